# Optimizing a Trainium2 kernel written in Bass

```python
import math
import jax
import jax.numpy as jnp
from jax import lax
import numpy as np

D_MODEL = 1024
BATCH = 32
SEQ = 2048
DEPTH = 1
DEC_BATCH = 8
DEC_SEQ = 16
PAST_LEN = 2048

CHUNK = 64
N_HEADS_A = 8
DK_A = 128
DV_A = 128
CONV_W = 4
N_HEADS_B = 8
DK_B = 128
DV_B = 128
HGRN_BLOCK = 16
D_FF = -(-8 * D_MODEL // (3 * 256)) * 256
WA_QK = N_HEADS_A * DK_A
WA_V = N_HEADS_A * DV_A
WB_K = N_HEADS_B * DK_B
WB_V = N_HEADS_B * DV_B
CONV_CH = 2 * WA_QK + WA_V
IN_SIZES = (WA_QK, WA_QK, WA_V, WA_V, N_HEADS_A, N_HEADS_A, WB_K, WB_K, WB_V, WB_V, D_MODEL, D_MODEL)
IN_TOTAL = sum(IN_SIZES)
ALPHA = (2.0 * DEPTH) ** 0.25
BETA_INIT = (8.0 * DEPTH) ** -0.25
LN_EPS = 1e-5
RMS_EPS = 1e-6
L2_EPS = 1e-6

kernel_name = 'hybrid_gdn_hgrn2_streaming_step'


def _split_points():
    return [int(s) for s in np.cumsum(IN_SIZES)[:-1]]


def _layer_norm(x, g, b):
    xf = x.astype(jnp.float32)
    mu = jnp.mean(xf, -1, keepdims=True)
    var = jnp.mean(jnp.square(xf - mu), -1, keepdims=True)
    y = (xf - mu) * lax.rsqrt(var + LN_EPS) * g.astype(jnp.float32) + b.astype(jnp.float32)
    return y.astype(x.dtype)


def _rms_heads(o, w):
    return o * lax.rsqrt(jnp.mean(o * o, -1, keepdims=True) + RMS_EPS) * w.astype(jnp.float32)


def _l2norm(x):
    return x * lax.rsqrt(jnp.sum(x * x, -1, keepdims=True) + L2_EPS)


def _causal_conv(u, buf, w):
    L = u.shape[1]
    up = jnp.concatenate([buf, u], axis=1)
    y = up[:, 0:L] * w[0]
    for j in range(1, CONV_W):
        y = y + up[:, j:j + L] * w[j]
    return jax.nn.silu(y), up[:, -(CONV_W - 1):]


def _to_blocks(t, c):
    B, L = t.shape[:2]
    n = -(-L // c)
    t = jnp.pad(t, [(0, 0), (0, n * c - L)] + [(0, 0)] * (t.ndim - 2))
    t = t.reshape((B, n, c) + t.shape[2:])
    return jnp.swapaxes(jnp.moveaxis(t, 3, 2), 0, 1)


def _from_blocks(o, L):
    o = jnp.moveaxis(jnp.swapaxes(o, 0, 1), 2, 3)
    B, n, c = o.shape[:3]
    return o.reshape((B, n * c) + o.shape[3:])[:, :L]


def _gated_delta_chunked(q, k, v, g, beta, s0):
    L = q.shape[1]
    c = CHUNK
    qb, kb, vb = _to_blocks(q, c), _to_blocks(k, c), _to_blocks(v, c)
    gb, bb = _to_blocks(g, c), _to_blocks(beta, c)
    incl = jnp.tril(jnp.ones((c, c), dtype=bool))
    strict = jnp.tril(jnp.ones((c, c), dtype=bool), -1)
    gc = jnp.cumsum(gb, axis=-1)
    diff = gc[..., :, None] - gc[..., None, :]
    decay = jnp.where(incl, jnp.exp(jnp.where(incl, diff, 0.0)), 0.0)
    k_beta = kb * bb[..., None]
    v_beta = vb * bb[..., None]
    lmat = jnp.where(strict, jnp.einsum('nbhtk,nbhsk->nbhts', k_beta, kb) * decay, 0.0)
    rhs = jnp.concatenate([v_beta, k_beta * jnp.exp(gc)[..., None]], axis=-1)
    sol = lax.linalg.triangular_solve(lmat, rhs, left_side=True, lower=True, unit_diagonal=True)
    value, k_cum = sol[..., :DV_A], sol[..., DV_A:]
    a_qk = jnp.einsum('nbhtk,nbhsk->nbhts', qb, kb) * decay
    q_dec = qb * jnp.exp(gc)[..., None]
    k_dec = kb * jnp.exp(gc[..., -1:] - gc)[..., None]
    last_dec = jnp.exp(gc[..., -1])

    def step(s, xs):
        qi, ki, vi, kci, ai, ld = xs
        v_new = vi - jnp.einsum('bhtk,bhkv->bhtv', kci, s)
        o = jnp.einsum('bhtk,bhkv->bhtv', qi, s) + jnp.einsum('bhts,bhsv->bhtv', ai, v_new)
        s = s * ld[..., None, None] + jnp.einsum('bhtk,bhtv->bhkv', ki, v_new)
        return s, o

    s_final, o = lax.scan(step, s0, (q_dec, k_dec, value, k_cum, a_qk, last_dec))
    return _from_blocks(o, L), s_final


def _hgrn2_chunked(q, k, v, logf, s0):
    L = q.shape[1]
    c = HGRN_BLOCK
    qb, kb, vb, fb = _to_blocks(q, c), _to_blocks(k, c), _to_blocks(v, c), _to_blocks(logf, c)
    incl = jnp.tril(jnp.ones((c, c), dtype=bool))[:, :, None]

    def step(s, xs):
        qi, ki, vi, lfi = xs
        bcum = jnp.cumsum(lfi, axis=2)
        blast = bcum[:, :, -1]
        diff = bcum[:, :, :, None, :] - bcum[:, :, None, :, :]
        dec = jnp.where(incl, jnp.exp(jnp.where(incl, diff, 0.0)), 0.0)
        a = jnp.einsum('bhtk,bhsk,bhtsk->bhts', qi, ki, dec)
        o = jnp.einsum('bhtk,bhkv->bhtv', qi * jnp.exp(bcum), s) + jnp.einsum('bhts,bhsv->bhtv', a, vi)
        s = s * jnp.exp(blast)[..., None] + jnp.einsum('bhsk,bhsv->bhkv', ki * jnp.exp(blast[:, :, None, :] - bcum), vi)
        return s, o

    s_final, o = lax.scan(step, s0, (qb, kb, vb, fb))
    return _from_blocks(o, L), s_final


def _token_mixers(x, conv_buf, s_gdn, s_hgrn, w_in, conv_w, a_log, dt_bias, gdn_norm_w, lb,
                  hgrn_norm_w, w_br_a, w_br_b, w_out):
    f32 = jnp.float32
    B, L, _ = x.shape
    proj = (x @ w_in).astype(f32)
    qa, ka, va, ga, aa, ba, qh, fh, ih, gh, mga, mgb = jnp.split(proj, _split_points(), axis=-1)
    qkv, new_buf = _causal_conv(jnp.concatenate([qa, ka, va], axis=-1), conv_buf.astype(f32), conv_w.astype(f32))
    qa, ka, va = jnp.split(qkv, [WA_QK, 2 * WA_QK], axis=-1)
    qa = _l2norm(qa.reshape(B, L, N_HEADS_A, DK_A)) * DK_A ** -0.5
    ka = _l2norm(ka.reshape(B, L, N_HEADS_A, DK_A))
    va = va.reshape(B, L, N_HEADS_A, DV_A)
    g = -jnp.exp(a_log.astype(f32)) * jax.nn.softplus(aa + dt_bias.astype(f32))
    beta = jax.nn.sigmoid(ba)
    oa, s_gdn_new = _gated_delta_chunked(qa, ka, va, g, beta, s_gdn.astype(f32))
    oa = _rms_heads(oa, gdn_norm_w) * jax.nn.silu(ga.reshape(B, L, N_HEADS_A, DV_A))
    z = fh.reshape(B, L, N_HEADS_B, DK_B)
    lbf = lb.astype(f32)
    logf = jnp.log(lbf + (1.0 - lbf) * jax.nn.sigmoid(z))
    kh = (1.0 - lbf) * jax.nn.sigmoid(-z)
    qh = jax.nn.silu(qh.reshape(B, L, N_HEADS_B, DK_B)) * DK_B ** -0.5
    vh = ih.reshape(B, L, N_HEADS_B, DV_B)
    ob, s_hgrn_new = _hgrn2_chunked(qh, kh, vh, logf, s_hgrn.astype(f32))
    ob = _rms_heads(ob, hgrn_norm_w) * jax.nn.sigmoid(gh.reshape(B, L, N_HEADS_B, DV_B))
    ya = oa.reshape(B, L, WA_V).astype(x.dtype) @ w_br_a
    yb = ob.reshape(B, L, WB_V).astype(x.dtype) @ w_br_b
    merged = jax.nn.sigmoid(mga).astype(x.dtype) * ya + jax.nn.sigmoid(mgb).astype(x.dtype) * yb
    return (merged @ w_out, new_buf.astype(x.dtype), s_gdn_new.astype(x.dtype), s_hgrn_new.astype(x.dtype))


def _layer(x, conv_buf, s_gdn, s_hgrn, w_in, conv_w, a_log, dt_bias, gdn_norm_w, lb, hgrn_norm_w,
           w_br_a, w_br_b, w_out, ln1_g, ln1_b, w_gate_up, w_down, ln2_g, ln2_b):
    mix, new_buf, s_gdn_new, s_hgrn_new = _token_mixers(x, conv_buf, s_gdn, s_hgrn, w_in, conv_w, a_log, dt_bias,
                                                        gdn_norm_w, lb, hgrn_norm_w, w_br_a, w_br_b, w_out)
    x = _layer_norm(ALPHA * x + mix, ln1_g, ln1_b)
    gate, up = jnp.split(x @ w_gate_up, 2, axis=-1)
    x = _layer_norm(ALPHA * x + (jax.nn.silu(gate) * up) @ w_down, ln2_g, ln2_b)
    return x, new_buf, s_gdn_new, s_hgrn_new


def setup_inputs(seed: int = 0) -> dict:
    key = jax.random.key(seed)
    ks = jax.random.split(key, 24)

    def nrm(k, shape, s):
        return jax.random.normal(k, shape, jnp.float32) * s

    dt = jnp.exp(jax.random.uniform(ks[8], (DEPTH, N_HEADS_A), jnp.float32, math.log(1e-3), math.log(1e-1)))
    return {
        'x_prompt': nrm(ks[0], (BATCH, SEQ, D_MODEL), 1.0),
        'x_sample': nrm(ks[1], (DEC_BATCH, DEC_SEQ, D_MODEL), 1.0),
        'cache_gdn_conv': nrm(ks[2], (DEPTH, DEC_BATCH, CONV_W - 1, CONV_CH), 1.0),
        'state_gdn': nrm(ks[3], (DEPTH, DEC_BATCH, N_HEADS_A, DK_A, DV_A), 0.1),
        'state_hgrn': nrm(ks[4], (DEPTH, DEC_BATCH, N_HEADS_B, DK_B, DV_B), 0.1),
        'w_in': nrm(ks[5], (DEPTH, D_MODEL, IN_TOTAL), D_MODEL ** -0.5),
        'conv_w': nrm(ks[6], (DEPTH, CONV_W, CONV_CH), CONV_W ** -0.5),
        'a_log': jnp.log(jax.random.uniform(ks[7], (DEPTH, N_HEADS_A), jnp.float32, 1.0, 16.0)),
        'dt_bias': dt + jnp.log(-jnp.expm1(-dt)),
        'gdn_norm_w': 1.0 + nrm(ks[9], (DEPTH, DV_A), 0.02),
        'hgrn_lb_logits': nrm(ks[10], (DEPTH + 1, WB_K), 0.5),
        'hgrn_norm_w': 1.0 + nrm(ks[11], (DEPTH, DV_B), 0.02),
        'w_br_a': nrm(ks[12], (DEPTH, WA_V, D_MODEL), WA_V ** -0.5),
        'w_br_b': nrm(ks[13], (DEPTH, WB_V, D_MODEL), WB_V ** -0.5),
        'w_out': nrm(ks[14], (DEPTH, D_MODEL, D_MODEL), BETA_INIT * D_MODEL ** -0.5),
        'ln1_g': 1.0 + nrm(ks[15], (DEPTH, D_MODEL), 0.02),
        'ln1_b': nrm(ks[16], (DEPTH, D_MODEL), 0.02),
        'w_gate_up': nrm(ks[17], (DEPTH, D_MODEL, 2 * D_FF), D_MODEL ** -0.5),
        'w_down': nrm(ks[18], (DEPTH, D_FF, D_MODEL), BETA_INIT * D_FF ** -0.5),
        'ln2_g': 1.0 + nrm(ks[19], (DEPTH, D_MODEL), 0.02),
        'ln2_b': nrm(ks[20], (DEPTH, D_MODEL), 0.02),
    }


def reference(x_prompt, x_sample, cache_gdn_conv, state_gdn, state_hgrn, w_in, conv_w, a_log, dt_bias,
              gdn_norm_w, hgrn_lb_logits, hgrn_norm_w, w_br_a, w_br_b, w_out, ln1_g, ln1_b, w_gate_up,
              w_down, ln2_g, ln2_b):
    B = x_prompt.shape[0]
    dt = x_prompt.dtype
    lb_all = jnp.cumsum(jax.nn.softmax(hgrn_lb_logits.astype(jnp.float32), axis=0), axis=0)
    zero_conv = jnp.zeros((B, CONV_W - 1, CONV_CH), dt)
    zero_gdn = jnp.zeros((B, N_HEADS_A, DK_A, DV_A), dt)
    zero_hgrn = jnp.zeros((B, N_HEADS_B, DK_B, DV_B), dt)
    y_prompt, y_sample = x_prompt, x_sample
    conv_p, gdn_p, hgrn_p, conv_s, gdn_s, hgrn_s = [], [], [], [], [], []
    for l in range(DEPTH):
        wl = (w_in[l], conv_w[l], a_log[l], dt_bias[l], gdn_norm_w[l], lb_all[l].reshape(N_HEADS_B, DK_B),
              hgrn_norm_w[l], w_br_a[l], w_br_b[l], w_out[l], ln1_g[l], ln1_b[l], w_gate_up[l], w_down[l],
              ln2_g[l], ln2_b[l])
        y_prompt, cp, gp, hp = _layer(y_prompt, zero_conv, zero_gdn, zero_hgrn, *wl)
        y_sample, cs, gs, hs = _layer(y_sample, cache_gdn_conv[l], state_gdn[l], state_hgrn[l], *wl)
        conv_p.append(cp)
        gdn_p.append(gp)
        hgrn_p.append(hp)
        conv_s.append(cs)
        gdn_s.append(gs)
        hgrn_s.append(hs)
    return (y_prompt, y_sample, jnp.stack(conv_p), jnp.stack(gdn_p), jnp.stack(hgrn_p),
            jnp.stack(conv_s), jnp.stack(gdn_s), jnp.stack(hgrn_s))
```

```python
import numpy as np
import concourse.bass as bass
import concourse.mybir as mybir
from concourse.bass_utils import run_bass_kernel_spmd

F32 = mybir.dt.float32
BF16 = mybir.dt.bfloat16
F32R = mybir.dt.float32r
AF = mybir.ActivationFunctionType
ALU = mybir.AluOpType

P = 128
D = 1024
NH = 8
DFF = 2816
NFF = 22
IN_TOTAL = 10256
OFF_QA, OFF_KA, OFF_VA, OFF_GA, OFF_AB = 0, 1024, 2048, 3072, 4096
OFF_QB, OFF_FB, OFF_IB, OFF_GB, OFF_MGA, OFF_MGB = 4112, 5136, 6160, 7184, 8208, 9232
ALPHA = 2.0 ** 0.25
LN_EPS = 1e-5
RMS_EPS = 1e-6
L2_EPS = 1e-6
GCH = 64
HBLK = 64
NEGBIG = -200.0

C_IDENT, C_SLT, C_TRI, C_NM, C_OFFD, C_CTRI, C_CREV, C_BDH, C_ONES, C_NBI = [i * 128 for i in range(10)]
C_CM = 1280
C_RST = 1282
NCONST = C_RST + 512
PR_CW, PR_L0, PR_L1, PR_GNW, PR_HNW, PR_ALOG, PR_DTB = 0, 96, 104, 112, 113, 114, 122
NPAR = 130


def make_consts():
    c = np.zeros((128, NCONST), np.float32)
    j = np.arange(128)[:, None]
    t = np.arange(128)[None, :]
    same_g = (j // GCH) == (t // GCH)
    same_h = (j // HBLK) == (t // HBLK)
    c[:, C_IDENT:C_IDENT + 128] = (j == t)
    c[:, C_SLT:C_SLT + 128] = (j > t)
    c[:, C_TRI:C_TRI + 128] = (j <= t)
    c[:, C_NM:C_NM + 128] = ~((t >= j) & same_g)
    c[:, C_OFFD:C_OFFD + 128] = (j != t)
    c[:, C_CTRI:C_CTRI + 128] = (j <= t) & same_g
    c[:, C_CREV:C_CREV + 128] = (j > t) & same_g
    c[:, C_BDH:C_BDH + 128] = (t >= j) & same_h
    c[:, C_ONES:C_ONES + 128] = 1.0
    c[:, C_NBI:C_NBI + 128] = NEGBIG * (j == t)
    c[:, C_CM] = (np.arange(128) < GCH)
    c[:, C_CM + 1] = (np.arange(128) >= GCH)
    c[:, C_RST:C_RST + 512] = ((np.arange(512) % HBLK) != 0)[None, :]
    return c


class Ev:
    __slots__ = ("sem", "val", "eng")

    def __init__(self, sem, val, eng):
        self.sem, self.val, self.eng = sem, val, eng


class T:
    def __init__(self, h, name, c0=0, hr=None):
        self.h, self.hr, self.name, self.c0 = h, hr, name, c0
        self.rclass = hr is not None
        self.w = None
        self.r = {}
        self.sem = None
        self.semcnt = 0
        self.const = False
        self.pend = 0
        self.bank = None

    def v(self, a=None, b=None, r0=0, r1=128, r=False):
        h = self.hr if r else self.h
        return V(self, h[r0:r1, self.c0 + a:self.c0 + b], r)

    def v3(self, inner, k0, k1, a, b, r0=0, r1=128, r=False):
        h = self.hr if r else self.h
        ap = h[r0:r1, :].rearrange("p (k c) -> p k c", c=inner)
        return V(self, ap[:, k0:k1, a:b], r)


class V:
    __slots__ = ("t", "ap", "r")

    def __init__(self, t, ap, r=False):
        self.t, self.ap, self.r = t, ap, r


def chk(eng, out):
    return


class Prog:
    ENG = ("pe", "act", "dve", "pool", "sp")

    def __init__(self, nc):
        self.nc = nc
        self.eobj = {"pe": nc.tensor, "act": nc.scalar, "dve": nc.vector, "pool": nc.gpsimd, "sp": nc.sync}
        self.stream = {e: [] for e in self.ENG}
        self.cnt = {e: 0 for e in self.ENG}
        self.esem = {e: nc.alloc_semaphore(f"prog_{e}") for e in self.ENG}
        self.known = {e: {} for e in self.ENG}
        self.pend = {e: [] for e in self.ENG}
        self.nsem = 5
        self.ninst = {e: 0 for e in self.ENG}
        self.final = []
        self.sym = {e: [] for e in self.ENG}

    def _waits(self, eng, reads, writes):
        need = {}

        def add(ev, raw):
            if ev is None:
                return
            if ev.eng == eng and not raw:
                return
            if self.known[eng].get(id(ev.sem), 0) >= ev.val:
                return
            k = id(ev.sem)
            if k not in need or need[k].val < ev.val:
                need[k] = ev

        for t in reads:
            assert t.pend == 0, f"read of tile {t.name} with unsignalled writer"
            add(t.w, True)
        for t in writes:
            add(t.w, False)
            for ev in t.r.values():
                add(ev, False)
        eo = self.eobj[eng]
        for ev in need.values():
            self.known[eng][id(ev.sem)] = ev.val
            self.stream[eng].append((lambda eo=eo, s=ev.sem, v=ev.val: eo.wait_ge(s, v)))
            self.sym[eng].append(("wait", id(ev.sem), ev.val))
            self.ninst[eng] += 1

    def _record(self, ev, reads, writes):
        for t in reads:
            if not t.const:
                k = id(ev.sem)
                if k not in t.r or t.r[k].val < ev.val:
                    t.r[k] = ev
        for t in writes:
            t.w = ev
            t.r = {}

    def op(self, eng, fn, reads=(), writes=(), inc=True):
        reads = [t for t in reads if t is not None]
        writes = [t for t in writes if t is not None]
        for t in list(reads) + list(writes):
            if t.bank is not None and t.bank not in writes:
                writes.append(t.bank)
        self._waits(eng, reads, writes)
        self.ninst[eng] += 1
        if inc:
            self.cnt[eng] += 1
            ev = Ev(self.esem[eng], self.cnt[eng], eng)
            sem = self.esem[eng]
            self.stream[eng].append(lambda: fn().then_inc(sem, 1))
            self.sym[eng].append(("inc", id(sem), 1))
            for (pr, pw) in self.pend[eng]:
                self._record(ev, pr, pw)
                for t in pw:
                    t.pend -= 1
            self.pend[eng] = []
            self._record(ev, reads, writes)
        else:
            self.stream[eng].append(fn)
            for t in writes:
                t.pend += 1
            self.pend[eng].append((reads, writes))

    def dma(self, q, out_ap, in_ap, reads=(), writes=(), semtile=None, final=False, **kw):
        nc = self.nc
        self._waits(q, list(reads), list(writes))
        st = semtile if semtile is not None else (writes[0] if writes else reads[0])
        if st.sem is None:
            st.sem = nc.alloc_semaphore(f"dsem{self.nsem}")
            self.nsem += 1
        st.semcnt += 16
        ev = Ev(st.sem, st.semcnt, "dma")
        eo = self.eobj[q]
        sem = st.sem
        self.stream[q].append(lambda: eo.dma_start(out=out_ap, in_=in_ap, **kw).then_inc(sem, 16))
        self.sym[q].append(("inc", id(sem), 16))
        self.ninst[q] += 1
        self._record(ev, list(reads), list(writes))
        if final:
            self.final.append(ev)

    def finish(self):
        eo = self.eobj["sp"]
        last = {}
        for ev in self.final:
            k = id(ev.sem)
            if k not in last or last[k].val < ev.val:
                last[k] = ev
        for ev in last.values():
            self.stream["sp"].append((lambda s=ev.sem, v=ev.val: eo.wait_ge(s, v)))
            self.sym["sp"].append(("wait", id(ev.sem), ev.val))

    def check(self):
        val = {}
        pc = {e: 0 for e in self.ENG}
        prog = True
        while prog:
            prog = False
            for e in self.ENG:
                st = self.sym[e]
                while pc[e] < len(st):
                    k, sid, v = st[pc[e]]
                    if k == "wait":
                        if val.get(sid, 0) < v:
                            break
                    else:
                        val[sid] = val.get(sid, 0) + v
                    pc[e] += 1
                    prog = True
        stuck = {e: (pc[e], len(self.sym[e]), self.sym[e][pc[e]], val.get(self.sym[e][pc[e]][1], 0))
                 for e in self.ENG if pc[e] < len(self.sym[e])}
        return stuck

    def emit(self):
        nc = self.nc
        with nc.Block() as block:
            @block.tensor
            def _(e):
                for f in self.stream["pe"]:
                    f()

            @block.scalar
            def _(e):
                for f in self.stream["act"]:
                    f()

            @block.vector
            def _(e):
                for f in self.stream["dve"]:
                    f()

            @block.gpsimd
            def _(e):
                for f in self.stream["pool"]:
                    f()

            @block.sync
            def _(e):
                for f in self.stream["sp"]:
                    f()


class Cfg:
    def __init__(self, nseq=4, seq=2048, g=512, ns=16, phase=99):
        self.nseq, self.seq, self.g, self.ns, self.phase = nseq, seq, g, ns, phase


class StopBuild(Exception):
    pass


def section_list():
    secs = []
    for h in range(NH):
        secs.append((("gdn", h), [("w_in", 0, 8, o + 128 * h) for o in (OFF_QA, OFF_KA, OFF_VA, OFF_GA)]))
    for h in range(NH):
        secs.append((("hgrn", h), [("w_in", 0, 8, o + 128 * h) for o in (OFF_QB, OFF_FB, OFF_IB, OFF_GB)]))
    for c in range(8):
        secs.append((("mrg", c), [("w_in", 0, 8, OFF_MGA + 128 * c), ("w_in", 0, 8, OFF_MGB + 128 * c),
                                  ("w_br_a", 0, 8, 128 * c), ("w_br_b", 0, 8, 128 * c)]))
    for ch in range(2):
        secs.append((("wout", ch), [("w_out", 0, 8, 512 * ch + 128 * p) for p in range(4)]))
    for s in range(NFF // 2):
        secs.append((("up", s), [("w_gate_up", 0, 8, 128 * (2 * s)), ("w_gate_up", 0, 8, DFF + 128 * (2 * s)),
                                 ("w_gate_up", 0, 8, 128 * (2 * s + 1)), ("w_gate_up", 0, 8, DFF + 128 * (2 * s + 1))]))
    for ch in range(2):
        for r in range(3):
            nk = 8 if r < 2 else NFF - 16
            secs.append((("down", ch, r), [("w_down", 1024 * r, nk, 512 * ch + 128 * p) for p in range(4)]))
    return secs


def build(cfg):
    nc = bass.Bass("TRN2", target_bir_lowering=False)
    G = cfg.g
    NT = cfg.nseq * cfg.seq
    NS = cfg.ns

    def din(name, shape):
        return nc.dram_tensor(name, list(shape), F32, kind="ExternalInput").ap()

    def dout(name, shape):
        return nc.dram_tensor(name, list(shape), F32, kind="ExternalOutput").ap()

    xp = din("xp", (NT, D))
    xs = din("xs", (NS, D))
    convbuf = din("convbuf", (P, 72))
    sgdn_in = din("sgdn", (NH, P, P))
    shgrn_in = din("shgrn", (NH, P, P))
    W = {
        "w_in": din("w_in", (D, IN_TOTAL)),
        "w_br_a": din("w_br_a", (D, D)),
        "w_br_b": din("w_br_b", (D, D)),
        "w_out": din("w_out", (D, D)),
        "w_gate_up": din("w_gate_up", (D, 2 * DFF)),
        "w_down": din("w_down", (DFF, D)),
    }
    params_d = din("params", (P, NPAR))
    lnp_d = din("lnp", (P, 4 * D))
    consts_d = din("consts", (P, NCONST))

    yp = dout("yp", (NT, D))
    ys = dout("ys", (NS, D))
    convp = dout("convp", (cfg.nseq, P, 72))
    gdnp = dout("gdnp", (cfg.nseq, NH, P, P))
    hgrnp = dout("hgrnp", (cfg.nseq, NH, P, P))
    convs = dout("convs", (1, P, 72))
    gdns = dout("gdns", (1, NH, P, P))
    hgrns = dout("hgrns", (1, NH, P, P))

    dbg = dout("dbg", (16, P, 512)) if cfg.phase < 99 else None
    secs = section_list()
    NSEC = len(secs)
    wscr = nc.dram_tensor("wscr", [NSEC, P, 4096], BF16).ap()

    pg = Prog(nc)
    _n = [0]

    def sb(shape_w, dt=F32, name=None, r=False):
        _n[0] += 1
        nm = f"{name or 'sb'}{_n[0]}"
        if r:
            dt = BF16
        h = nc.alloc_sbuf_tensor(nm, [P, shape_w], dt)
        t_ = T(h, nm, 0, h if r else None)
        t_.rclass = False
        return t_

    banks = [nc.alloc_psum_tensor(f"bank{i}", [P, 512], F32) for i in range(8)]
    block_ = [T(None, f"banklock{i}") for i in range(8)]

    banks_b = [b_.bitcast(BF16) for b_ in banks]

    def pst(bi, name, c0=0):
        t = T(banks[bi], name, c0)
        t.bank = block_[bi]
        t.hb = banks_b[bi]
        return t

    def vb(t, a, b, r0=0, r1=128):
        return V(t, t.hb[r0:r1, 2 * t.c0 + a:2 * t.c0 + b])

    acc = [pst(i, f"acc{i}") for i in range(3)]
    acc_pools = {"dense": [0, 1, 2], "pre": [0, 1], "rms": [2]}
    acc_i = {"dense": 0, "pre": 0, "rms": 0}
    acc_mode = ["dense"]

    def next_acc(pool=None):
        pool = pool or acc_mode[0]
        lst = acc_pools[pool]
        t = acc[lst[acc_i[pool] % len(lst)]]
        acc_i[pool] += 1
        return t

    NCORE = 5
    ps_o = [pst(3, f"pso{k}", 64 * k) for k in range(NCORE)]
    core_bank = [4, 5, 6, 7, 2]
    ps_m = [[pst(core_bank[k], f"psm{k}_{q}", 128 * q) for q in range(4)] for k in range(NCORE)]

    consts = sb(NCONST, name="consts")
    params = sb(NPAR, name="params")
    lnp = sb(2 * D, name="lnp")
    identr = sb(128, name="identr", r=True)
    onesr = sb(128, name="onesr", r=True)
    pg.dma("sp", consts.h[:, :], consts_d, writes=[consts])
    pg.dma("sp", params.h[:, :], params_d, writes=[params])

    def rd(*vs):
        return [v.t for v in vs if isinstance(v, V)]

    def apx(x):
        return x.ap if isinstance(x, V) else x

    def mm(out, lhsT, rhs, start=True, stop=True, inc=None, tp=None):
        kw = {}
        if tp is not None:
            kw["tile_position"] = tp
        pg.op("pe", lambda: nc.tensor.matmul(out.ap, lhsT.ap, rhs.ap, start=start, stop=stop, **kw),
              reads=rd(lhsT, rhs), writes=[out.t], inc=(stop if inc is None else inc))

    def tr(out, in_, ident):
        pg.op("pe", lambda: nc.tensor.transpose(out.ap, in_.ap, ident.ap), reads=rd(in_, ident), writes=[out.t])

    def act(out, in_, func, bias=0.0, scale=1.0, accum=None):
        kw = {}
        if accum is not None:
            kw["accum_out"] = accum.ap
        wr = [out.t] + ([accum.t] if accum is not None else [])
        chk("act", out)
        pg.op("act", lambda: nc.scalar.activation(out=out.ap, in_=in_.ap, func=func, bias=apx(bias), scale=apx(scale), **kw),
              reads=rd(in_, bias, scale), writes=wr)

    def ts(eng, out, in0, s1, s2, op0, op1=None):
        eo = pg.eobj[eng]
        chk(eng, out)
        if op1 is None:
            pg.op(eng, lambda: eo.tensor_scalar(out=out.ap, in0=in0.ap, scalar1=apx(s1), scalar2=None, op0=op0),
                  reads=rd(in0, s1), writes=[out.t])
        else:
            pg.op(eng, lambda: eo.tensor_scalar(out=out.ap, in0=in0.ap, scalar1=apx(s1), scalar2=apx(s2), op0=op0, op1=op1),
                  reads=rd(in0, s1, s2), writes=[out.t])

    def tt(eng, out, a, b, op):
        eo = pg.eobj[eng]
        chk(eng, out)
        pg.op(eng, lambda: eo.tensor_tensor(out=out.ap, in0=a.ap, in1=b.ap, op=op), reads=rd(a, b), writes=[out.t])

    def stt(out, in0, scalar, in1, op0, op1):
        chk("dve", out)
        pg.op("dve", lambda: nc.vector.scalar_tensor_tensor(out=out.ap, in0=in0.ap, scalar=apx(scalar), in1=in1.ap, op0=op0, op1=op1),
              reads=rd(in0, scalar, in1), writes=[out.t])

    def scan(out, d0, d1, init, op0, op1):
        pg.op("dve", lambda: nc.vector.tensor_tensor_scan(out=out.ap, data0=d0.ap, data1=d1.ap, initial=init, op0=op0, op1=op1),
              reads=rd(d0, d1), writes=[out.t])

    def recip(out, in_):
        pg.op("dve", lambda: nc.vector.reciprocal(out=out.ap, in_=in_.ap), reads=rd(in_), writes=[out.t])

    def cp(eng, out, in_):
        if eng == "act":
            act(out, in_, AF.Copy)
        else:
            eo = pg.eobj[eng]
            chk(eng, out)
            pg.op(eng, lambda: eo.tensor_copy(out=out.ap, in_=in_.ap), reads=rd(in_), writes=[out.t])

    def memset(eng, out, val):
        eo = pg.eobj[eng]
        chk(eng, out)
        pg.op(eng, lambda: eo.memset(out.ap, val), writes=[out.t])

    def sigmoid_chain(out, in_, tmp1, tmp2, n, rows=128):
        act(tmp1.v(0, n, 0, rows), in_, AF.Exp, scale=-1.0)
        act(tmp2.v(0, n, 0, rows), tmp1.v(0, n, 0, rows), AF.Ln, bias=1.0)
        act(out, tmp2.v(0, n, 0, rows), AF.Exp, scale=-1.0)

    cv = lambda a, b, r0=0, r1=128: consts.v(a, b, r0, r1)

    cp("dve", identr.v(0, 128, r=True), cv(C_IDENT, C_IDENT + 128))
    cp("dve", onesr.v(0, 128, r=True), cv(C_ONES, C_ONES + 128))

    dpar = sb(32, name="dpar")
    tt("dve", dpar.v(24, 32), params.v(PR_L1, PR_L1 + 8), params.v(PR_L0, PR_L0 + 8), ALU.subtract)
    act(dpar.v(24, 32), dpar.v(24, 32), AF.Exp)
    ts("dve", dpar.v(24, 32), dpar.v(24, 32), 1.0, None, ALU.add)
    recip(dpar.v(0, 8), dpar.v(24, 32))
    ts("dve", dpar.v(8, 16), dpar.v(0, 8), -1.0, 1.0, ALU.mult, ALU.add)
    act(dpar.v(16, 24), params.v(PR_ALOG, PR_ALOG + 8), AF.Exp)
    ts("dve", dpar.v(16, 24), dpar.v(16, 24), -1.0, None, ALU.mult)
    for t_ in (consts, params, identr, onesr):
        t_.const = True

    x_tm = [sb(D, name="xtm") for _ in range(max(1, G // 128))]
    NXT = len(x_tm)
    xT = [sb(G, BF16, name="xT") for _ in range(8)]
    NRING = 3
    wring = [sb(4096, BF16, name="wring") for _ in range(NRING)]
    wab = sb(128, BF16, name="wab")
    tmpFs = [[sb(G + 4, name="tmpF") for _ in range(6)] for _ in range(2)]
    tmpRs = [sb(G + 4, name="tmpR", r=True) for _ in range(2)]
    ubs = [[sb(G + 4, name="ub", r=True) for _ in range(3)] for _ in range(2)]
    dgs = [[sb(128, name="dg", r=True) for _ in range(4)] for _ in range(2)]
    tmpF = tmpFs[0]
    tmpR = tmpRs[0]
    store = [[dict(q=sb(G + 4, name="stq", r=True), k=sb(G + 4, name="stk", r=True),
                   v=sb(G + 4, name="stv", r=True), g=sb(G + 4, BF16, name="stg"), e=sb(16, name="ebl")) for k_ in range(NCORE)]
             for _ in range(2)]
    rmsR = sb(G + 4, name="rmsR", r=True)
    rms1 = sb(G + 4, name="rms1")
    smF = [[sb(128, name="smF") for _ in range(4)] for _ in range(NCORE)]
    smR = [[sb(128, name="smR", r=True) for _ in range(7)] for _ in range(NCORE)]
    tsc = [sb(96, name="tsc") for _ in range(NXT)]
    Sg = [sb(128, name="Sg") for _ in range(NH)]
    Sh = [sb(128, name="Sh") for _ in range(NH)]
    Sgb = [sb(128, name="Sgb", r=True) for _ in range(NH)]
    Shb = [sb(128, name="Shb", r=True) for _ in range(NH)]
    halo = sb(72, name="halo")
    sstage = sb(128, name="sstage")
    oaT = [sb(G, BF16, name="oaT") for _ in range(8)]
    obT = [sb(G, BF16, name="obT") for _ in range(8)]
    mgT = [sb(G, BF16, name="mgT") for _ in range(8)]
    actT = oaT + obT + mgT[:6]
    lnsc = [sb(16, name="lnsc") for _ in range(NXT)]
    lnjunk = sb(D, BF16, name="lnjunk")
    dtile = {n: T(None, "dram_" + str(n)) for n in range(NSEC)}

    cast_engs = ["dve", "pool", "act"]
    ci = 0
    stg_i = 0
    st = x_tm[stg_i % NXT]
    stg_i += 1
    pg.dma("sp", st.h[:, 0:128].rearrange("p (k c) -> p k c", c=16),
           W["w_in"][:, OFF_AB:OFF_AB + 16].rearrange("(k p) c -> p k c", p=128), writes=[st])
    cp("dve", wab.v(0, 128), st.v(0, 128))
    wab.const = True
    for si, (nm, pieces) in enumerate(secs):
        slot = wring[si % NRING]
        for pi, (wn, r0, nk, c0) in enumerate(pieces):
            st = x_tm[stg_i % NXT]
            stg_i += 1
            pg.dma("sp", st.h[:, 0:nk * 128].rearrange("p (k c) -> p k c", c=128),
                   W[wn][r0:r0 + nk * 128, c0:c0 + 128].rearrange("(k p) c -> p k c", p=128), writes=[st])
            cp(cast_engs[ci % 3], slot.v3(512, 0, nk, pi * 128, pi * 128 + 128), st.v3(128, 0, nk, 0, 128))
            ci += 1
        nk = pieces[0][2]
        pg.dma("pool", wscr[si, :, 0:nk * 512], slot.h[:, 0:nk * 512], reads=[slot], writes=[dtile[si]], semtile=dtile[si])

    sec_idx = {nm: i for i, (nm, _) in enumerate(secs)}
    use_seq = []
    loaded = [0]
    slot_of = {}

    def plan_group():
        for (nm, _) in secs:
            use_seq.append(sec_idx[nm])

    def ensure(k):
        while loaded[0] <= k and loaded[0] < len(use_seq):
            u = loaded[0]
            si = use_seq[u]
            slot = wring[u % NRING]
            assert id(slot) not in held, "weight ring slot still held by a pre stream"
            nk = secs[si][1][0][2]
            pg.dma("sp", slot.h[:, 0:nk * 512], wscr[si, :, 0:nk * 512], reads=[dtile[si]], writes=[slot])
            slot_of[u] = slot
            loaded[0] += 1

    use_ptr = [0]
    held = set()

    def next_sec(expect, lookahead=NRING - 1):
        u = use_ptr[0]
        assert secs[use_seq[u]][0] == expect, (secs[use_seq[u]][0], expect)
        ensure(min(u + lookahead, len(use_seq) - 1))
        use_ptr[0] += 1
        return slot_of[u]

    def next_secs(expects):
        u = use_ptr[0]
        ensure(min(u + NRING - 1, len(use_seq) - 1))
        out = []
        for e in expects:
            assert secs[use_seq[use_ptr[0]]][0] == e
            out.append(slot_of[use_ptr[0]])
            use_ptr[0] += 1
        return out

    def tiles_of(ntok):
        return [(t0, min(128, ntok - t0)) for t0 in range(0, ntok, 128)]

    def proj_fm(slot, piece, ntok, src, nk=8):
        a = next_acc()
        for kc in range(nk):
            mm(a.v(0, ntok), slot.v3(512, kc, kc + 1, piece * 128, piece * 128 + 128), src[kc].v(0, ntok),
               start=(kc == 0), stop=(kc == nk - 1))
        return a

    def rms_gate_out(osb, ntok, normw_col, gate_v, dst):
        sq, t1, t2 = rmsR, rms1, rms1
        act(sq.v(0, ntok, r=True), osb.v(0, ntok), AF.Square)
        a = next_acc("pre")
        mm(a.v(0, ntok), onesr.v(0, 128, r=True), sq.v(0, ntok, r=True))
        act(t1.v(0, ntok), a.v(0, ntok), AF.Ln, bias=RMS_EPS, scale=1.0 / 128)
        act(t1.v(0, ntok), t1.v(0, ntok), AF.Exp, scale=-0.5)
        tt("dve", t2.v(0, ntok), osb.v(0, ntok), t1.v(0, ntok), ALU.mult)
        stt(dst, t2.v(0, ntok), normw_col, gate_v, ALU.mult, ALU.mult)

    def run_cores(gens):
        gens = list(gens)
        while gens:
            for g_ in list(gens):
                try:
                    next(g_)
                except StopIteration:
                    gens.remove(g_)

    def emit_group(x_src, y_dst, ntok):
        tls = tiles_of(ntok)
        for ti, (t0, nt) in enumerate(tls):
            pg.dma("sp", x_tm[ti].h[0:nt, :], x_src[t0:t0 + nt, :], writes=[x_tm[ti]])
        for kc in range(8):
            a = next_acc()
            for ti, (t0, nt) in enumerate(tls):
                tr(a.v(t0, t0 + nt), x_tm[ti].v(kc * 128, kc * 128 + 128, 0, nt), cv(C_IDENT, C_IDENT + nt, 0, nt))
            cp("act" if kc % 2 else "dve", xT[kc].v(0, ntok), a.v(0, ntok))
        def stage_b():
            for ti, (t0, nt) in enumerate(tls):
                s = tsc[ti]
                a = ps_m[0][0]
                for kc in range(8):
                    mm(a.v(0, 16, 0, nt), xT[kc].v(t0, t0 + nt), wab.v(kc * 16, kc * 16 + 16), start=(kc == 0), stop=(kc == 7))
                tt("dve", s.v(64, 72, 0, nt), a.v(0, 8, 0, nt), params.v(PR_DTB, PR_DTB + 8, 0, nt), ALU.add)
                act(s.v(64, 72, 0, nt), s.v(64, 72, 0, nt), AF.Exp)
                act(s.v(64, 72, 0, nt), s.v(64, 72, 0, nt), AF.Ln, bias=1.0)
                tt("dve", s.v(0, 8, 0, nt), s.v(64, 72, 0, nt), dpar.v(16, 24, 0, nt), ALU.mult)
                act(s.v(72, 80, 0, nt), a.v(8, 16, 0, nt), AF.Exp, scale=-1.0)
                ts("dve", s.v(72, 80, 0, nt), s.v(72, 80, 0, nt), 1.0, None, ALU.add)
                recip(s.v(8, 16, 0, nt), s.v(72, 80, 0, nt))
                ts("dve", s.v(16, 24, 0, nt), s.v(8, 16, 0, nt), -1.0, None, ALU.mult)
                yield
                b = ps_m[0][1]
                mm(b.v(0, 8, 0, nt), cv(C_CTRI, C_CTRI + nt, 0, nt), s.v(0, 8, 0, nt))
                act(s.v(24, 32, 0, nt), b.v(0, 8, 0, nt), AF.Exp)
                ts("dve", s.v(32, 40, 0, nt), s.v(24, 32, 0, nt), -1.0, None, ALU.mult)
                c = ps_m[0][2]
                mm(c.v(0, 8, 0, nt), cv(C_CREV, C_CREV + nt, 0, nt), s.v(0, 8, 0, nt))
                act(s.v(40, 48, 0, nt), c.v(0, 8, 0, nt), AF.Exp)
                for ch in range(2):
                    ts("dve", s.v(48 + 8 * ch, 56 + 8 * ch, 0, nt), s.v(0, 8, 0, nt), cv(C_CM + ch, C_CM + ch + 1, 0, nt), None, ALU.mult)
                d = ps_m[0][3]
                mm(d.v(0, 16), cv(C_ONES, C_ONES + 128, 0, nt), s.v(48, 64, 0, nt))
                act(s.v(80, 96), d.v(0, 16), AF.Exp)
                yield

        if cfg.phase <= 2.0:
            raise StopBuild()
        tasks = [("g", h) for h in range(NH)] + [("h", h) for h in range(NH)]
        nw = len(tasks) // NCORE

        free_store = [store[p_][k_] for p_ in range(2) for k_ in range(NCORE)]
        free_core = list(range(NCORE))
        ready = []
        pre_act = [None, None]
        core_act = []
        nxt = [0]

        def mk_pre(task, st, ps_):
            kind, h = task
            return gdn_pre(h, st, ntok, ps_) if kind == "g" else hgrn_pre(h, st, ntok, tls, ps_)

        def mk_core(task, k, st):
            kind, h = task
            return gdn_core(h, k, st, ntok, tls) if kind == "g" else hgrn_core(h, k, st, ntok, tls)

        acc_mode[0] = "pre"
        sb_gen = [stage_b()]
        while True:
            if sb_gen[0] is not None:
                try:
                    next(sb_gen[0])
                except StopIteration:
                    sb_gen[0] = None
            for ps_ in range(2):
                if pre_act[ps_] is None and nxt[0] < len(tasks) and free_store:
                    st = free_store.pop(0)
                    task = tasks[nxt[0]]
                    nxt[0] += 1
                    pre_act[ps_] = (mk_pre(task, st, ps_), task, st)
            while ready and free_core and sb_gen[0] is None:
                k = free_core.pop(0)
                task, st = ready.pop(0)
                core_act.append((mk_core(task, k, st), k, st))
            if not core_act and pre_act[0] is None and pre_act[1] is None and sb_gen[0] is None:
                assert nxt[0] == len(tasks) and not ready
                break
            for ps_ in range(2):
                if pre_act[ps_] is not None:
                    g_, task, st = pre_act[ps_]
                    try:
                        next(g_)
                    except StopIteration:
                        ready.append((task, st))
                        pre_act[ps_] = None
            for item in list(core_act):
                g_, k, st = item
                try:
                    next(g_)
                except StopIteration:
                    core_act.remove(item)
                    free_core.append(k)
                    free_store.append(st)
        acc_mode[0] = "dense"
        if cfg.phase <= 4:
            raise StopBuild()
        B = tmpF
        for c in range(8):
            slot = next_sec(("mrg", c))
            p1 = proj_fm(slot, 0, ntok, xT)
            sigmoid_chain(B[0].v(0, ntok), p1.v(0, ntok), B[0], B[0], ntok)
            p2 = proj_fm(slot, 1, ntok, xT)
            sigmoid_chain(B[1].v(0, ntok), p2.v(0, ntok), B[1], B[1], ntok)
            p3 = proj_fm(slot, 2, ntok, oaT)
            tt("dve", B[2].v(0, ntok), p3.v(0, ntok), B[0].v(0, ntok), ALU.mult)
            p4 = proj_fm(slot, 3, ntok, obT)
            tt("dve", B[3].v(0, ntok), p4.v(0, ntok), B[1].v(0, ntok), ALU.mult)
            tt("dve", mgT[c].v(0, ntok), B[2].v(0, ntok), B[3].v(0, ntok), ALU.add)
        for ch in range(2):
            slot = next_sec(("wout", ch))
            for ti, (t0, nt) in enumerate(tls):
                a = next_acc()
                for kc in range(8):
                    mm(a.v(0, 512, 0, nt), mgT[kc].v(t0, t0 + nt), slot.v3(512, kc, kc + 1, 0, 512), start=(kc == 0), stop=(kc == 7))
                stt(x_tm[ti].v(ch * 512, ch * 512 + 512, 0, nt), x_tm[ti].v(ch * 512, ch * 512 + 512, 0, nt), ALPHA,
                    a.v(0, 512, 0, nt), ALU.mult, ALU.add)
        pg.dma("sp", lnp.h[:, :], lnp_d[:, 0:2 * D], writes=[lnp])
        for ti, (t0, nt) in enumerate(tls):
            layer_norm(x_tm[ti], nt, lnsc[ti], 0, None)
        if cfg.phase <= 5:
            raise StopBuild()
        for kc in range(8):
            a = next_acc()
            for ti, (t0, nt) in enumerate(tls):
                tr(a.v(t0, t0 + nt), x_tm[ti].v(kc * 128, kc * 128 + 128, 0, nt), cv(C_IDENT, C_IDENT + nt, 0, nt))
            cp("act" if kc % 2 else "dve", xT[kc].v(0, ntok), a.v(0, ntok))
        for s_ in range(NFF // 2):
            slot = next_sec(("up", s_))
            for jj in range(2):
                j = 2 * s_ + jj
                Bq = tmpF[3 * jj:3 * jj + 3]
                pg_ = proj_fm(slot, 2 * jj, ntok, xT)
                pu_ = proj_fm(slot, 2 * jj + 1, ntok, xT)
                sigmoid_chain(Bq[0].v(0, ntok), pg_.v(0, ntok), Bq[0], Bq[0], ntok)
                tt("dve", Bq[1].v(0, ntok), pg_.v(0, ntok), Bq[0].v(0, ntok), ALU.mult)
                tt("dve", actT[j].v(0, ntok), pu_.v(0, ntok), Bq[1].v(0, ntok), ALU.mult)
        for ch in range(2):
            slots = next_secs([("down", ch, r) for r in range(3)])
            for ti, (t0, nt) in enumerate(tls):
                a = next_acc()
                for j in range(NFF):
                    mm(a.v(0, 512, 0, nt), actT[j].v(t0, t0 + nt), slots[j // 8].v3(512, j % 8, j % 8 + 1, 0, 512),
                       start=(j == 0), stop=(j == NFF - 1))
                stt(x_tm[ti].v(ch * 512, ch * 512 + 512, 0, nt), x_tm[ti].v(ch * 512, ch * 512 + 512, 0, nt), ALPHA,
                    a.v(0, 512, 0, nt), ALU.mult, ALU.add)
        pg.dma("sp", lnp.h[:, :], lnp_d[:, 2 * D:4 * D], writes=[lnp])
        for ti, (t0, nt) in enumerate(tls):
            layer_norm(x_tm[ti], nt, lnsc[ti], 2, None)
            pg.dma("sp", y_dst[t0:t0 + nt, :], x_tm[ti].h[0:nt, :], reads=[x_tm[ti]], semtile=x_tm[ti], final=True)

    def layer_norm(xt, nt, sc, which, B):
        junk = lnjunk
        act(junk.v(0, D, 0, nt), xt.v(0, D, 0, nt), AF.Copy, accum=sc.v(0, 1, 0, nt))
        act(junk.v(0, D, 0, nt), xt.v(0, D, 0, nt), AF.Square, accum=sc.v(1, 2, 0, nt))
        ts("dve", sc.v(8, 9, 0, nt), sc.v(0, 1, 0, nt), 1.0 / D, None, ALU.mult)
        tt("dve", sc.v(9, 10, 0, nt), sc.v(8, 9, 0, nt), sc.v(8, 9, 0, nt), ALU.mult)
        stt(sc.v(10, 11, 0, nt), sc.v(1, 2, 0, nt), 1.0 / D, sc.v(9, 10, 0, nt), ALU.mult, ALU.subtract)
        act(sc.v(11, 12, 0, nt), sc.v(10, 11, 0, nt), AF.Ln, bias=LN_EPS)
        act(sc.v(11, 12, 0, nt), sc.v(11, 12, 0, nt), AF.Exp, scale=-0.5)
        stt(sc.v(12, 13, 0, nt), sc.v(8, 9, 0, nt), -1.0, sc.v(11, 12, 0, nt), ALU.mult, ALU.mult)
        act(xt.v(0, D, 0, nt), xt.v(0, D, 0, nt), AF.Identity, bias=sc.v(12, 13, 0, nt), scale=sc.v(11, 12, 0, nt))
        tt("dve", xt.v(0, D, 0, nt), xt.v(0, D, 0, nt), lnp.v(0, D, 0, nt), ALU.mult)
        tt("dve", xt.v(0, D, 0, nt), xt.v(0, D, 0, nt), lnp.v(D, 2 * D, 0, nt), ALU.add)

    def silu_from(out, y, t1, t2, n):
        sigmoid_chain(t2.v(0, n), y, t1, t2, n)
        tt("dve", out, y, t2.v(0, n), ALU.mult)

    def gdn_pre(h, st, ntok, ps_):
        slot = next_sec(("gdn", h), lookahead=0)
        held.add(id(slot))
        TF, TR, ub, dg = tmpFs[ps_], tmpRs[ps_], ubs[ps_], dgs[ps_]
        t1, t2, cq, ck = TF[0], TF[1], TF[2], TF[3]
        for i in range(3):
            a = proj_fm(slot, i, ntok, xT)
            ch = i * 8 + h
            cp("pool", ub[i].v(0, 3), halo.v(ch * 3, ch * 3 + 3))
            cp("act", ub[i].v(3, 3 + ntok), a.v(0, ntok))
            cp("dve", halo.v(ch * 3, ch * 3 + 3), a.v(ntok - 3, ntok))
            yield
        a = proj_fm(slot, 3, ntok, xT)
        held.discard(id(slot))
        sigmoid_chain(t2.v(0, ntok), a.v(0, ntok), t1, t2, ntok)
        tt("dve", st["g"].v(0, ntok), a.v(0, ntok), t2.v(0, ntok), ALU.mult)
        yield
        cvo = [cq, ck, st["v"]]
        for i in range(3):
            ch = i * 8 + h
            for j in range(4):
                ts("dve", dg[j].v(0, 128), identr.v(0, 128), params.v(PR_CW + ch * 4 + j, PR_CW + ch * 4 + j + 1), None, ALU.mult)
            yield
            a = next_acc()
            for j in range(4):
                mm(a.v(0, ntok), dg[j].v(0, 128), ub[i].v(j, j + ntok), start=(j == 0), stop=(j == 3))
            sigmoid_chain(t2.v(0, ntok), a.v(0, ntok), t1, t2, ntok)
            tt("dve", cvo[i].v(0, ntok), a.v(0, ntok), t2.v(0, ntok), ALU.mult)
            yield
        for src, dst, lb_ in ((cq, st["q"], -0.5 * np.log(128.0)), (ck, st["k"], 0.0)):
            act(TR.v(0, ntok, r=True), src.v(0, ntok), AF.Square)
            a = next_acc()
            mm(a.v(0, ntok), onesr.v(0, 128, r=True), TR.v(0, ntok, r=True))
            act(t1.v(0, ntok), a.v(0, ntok), AF.Ln, bias=L2_EPS)
            act(t1.v(0, ntok), t1.v(0, ntok), AF.Exp, scale=-0.5, bias=float(lb_))
            tt("dve", dst.v(0, ntok, r=True), src.v(0, ntok), t1.v(0, ntok), ALU.mult)
            yield

    def gdn_core(h, k, st, ntok, tls):
        qn, kn, vn = st["q"], st["k"], st["v"]
        SF, SR, PM, pso = smF[k], smR[k], ps_m[k], ps_o[k]
        S, Sb = Sg[h], Sgb[h]
        for ti, (t0, nt) in enumerate(tls):
            s = tsc[ti]
            col = lambda base, r0=0, r1=nt: s.v(base + h, base + h + 1, r0, r1)
            E, Es, lg, vtm = SF[0], SF[1], SF[2], SF[3]
            AT, Q, QT, X = SR[0], [SR[1], SR[2]], [SR[3], SR[4]], [SR[5], SR[6]]
            kdec, Wt, VNt, P1s = SR[1], SR[2], SR[3], SR[4]
            mm(PM[0].v(0, nt, 0, nt), kn.v(t0, t0 + nt, r=True), kn.v(t0, t0 + nt, r=True))
            mm(PM[1].v(0, nt, 0, nt), kn.v(t0, t0 + nt, r=True), qn.v(t0, t0 + nt, r=True))
            ts("dve", lg.v(0, nt, 0, nt), cv(C_SLT, C_SLT + nt, 0, nt), col(0), None, ALU.mult)
            yield
            mm(PM[2].v(0, nt, 0, nt), lg.v(0, nt, 0, nt), cv(C_TRI, C_TRI + nt, 0, nt), start=True, stop=False)
            mm(PM[2].v(0, nt, 0, nt), cv(C_NBI, C_NBI + nt, 0, nt), cv(C_NM, C_NM + nt, 0, nt), start=False, stop=True)
            act(E.v(0, nt, 0, nt), PM[2].v(0, nt, 0, nt), AF.Exp)
            yield
            tt("dve", Es.v(0, nt, 0, nt), E.v(0, nt, 0, nt), cv(C_OFFD, C_OFFD + nt, 0, nt), ALU.mult)
            tt("dve", AT.v(0, nt, 0, nt, r=True), PM[1].v(0, nt, 0, nt), E.v(0, nt, 0, nt), ALU.mult)
            yield
            stt(Q[0].v(0, nt, 0, nt, r=True), PM[0].v(0, nt, 0, nt), col(16), Es.v(0, nt, 0, nt), ALU.mult, ALU.mult)
            yield
            tr(vb(PM[3], 0, nt, 0, nt), Q[0].v(0, nt, 0, nt), identr.v(0, nt, 0, nt))
            tt("dve", X[0].v(0, nt, 0, nt, r=True), Q[0].v(0, nt, 0, nt), cv(C_IDENT, C_IDENT + nt, 0, nt), ALU.add)
            yield
            cp("act", QT[0].v(0, nt, 0, nt, r=True), vb(PM[3], 0, nt, 0, nt))
            yield
            cur = 0
            for lvl in range(5):
                nx = 1 - cur
                if lvl < 4:
                    mm(PM[0].v(0, nt, 0, nt), QT[cur].v(0, nt, 0, nt, r=True), Q[cur].v(0, nt, 0, nt, r=True))
                mm(PM[1].v(0, nt, 0, nt), Q[cur].v(0, nt, 0, nt, r=True), QT[cur].v(0, nt, 0, nt, r=True))
                yield
                if lvl < 4:
                    cp("act", Q[nx].v(0, nt, 0, nt, r=True), PM[0].v(0, nt, 0, nt))
                cp("dve", QT[nx].v(0, nt, 0, nt, r=True), PM[1].v(0, nt, 0, nt))
                yield
                mm(PM[2].v(0, nt, 0, nt), QT[nx].v(0, nt, 0, nt, r=True), X[cur].v(0, nt, 0, nt, r=True))
                yield
                tt("dve", X[nx].v(0, nt, 0, nt, r=True), X[cur].v(0, nt, 0, nt), PM[2].v(0, nt, 0, nt), ALU.add)
                yield
                cur = nx
            Xf = X[cur]
            tr(vb(PM[3], 0, 128, 0, nt), kn.v(t0, t0 + nt), identr.v(0, 128))
            tr(vb(PM[0], 0, 128, 0, nt), vn.v(t0, t0 + nt), identr.v(0, 128))
            yield
            act(kdec.v(0, 128, 0, nt, r=True), vb(PM[3], 0, 128, 0, nt), AF.Copy, scale=col(40))
            cp("dve", vtm.v(0, 128, 0, nt), vb(PM[0], 0, 128, 0, nt))
            yield
            chunks = [(r0, min(nt, r0 + GCH)) for r0 in range(0, nt, GCH)]
            for ci_, (r0, r1) in enumerate(chunks):
                tp = (r0, 0) if nt > GCH else None
                cw_ = r1 - r0
                memset("dve", pso.v(0, cw_), 0.0)
                mm(PM[1].v(0, 128, 0, nt), kn.v(t0, t0 + nt, r=True), Sb.v(0, 128))
                mm(PM[3].v(0, 128, 0, nt), qn.v(t0, t0 + nt, r=True), Sb.v(0, 128))
                yield
                stt(Wt.v(0, 128, r0, r1, r=True), PM[1].v(0, 128, r0, r1), col(32, r0, r1), vtm.v(0, 128, r0, r1), ALU.mult, ALU.add)
                act(P1s.v(0, 128, r0, r1, r=True), PM[3].v(0, 128, r0, r1), AF.Copy, scale=col(24, r0, r1))
                yield
                mm(PM[2].v(0, 128, 0, nt), Xf.v(0, nt, r0, r1, r=True), Wt.v(0, 128, r0, r1, r=True), tp=tp)
                mm(pso.v(0, cw_), P1s.v(0, 128, r0, r1, r=True), identr.v(r0, r1, r0, r1, r=True), start=False, stop=False, inc=True, tp=tp)
                yield
                act(VNt.v(0, 128, r0, r1, r=True), PM[2].v(0, 128, r0, r1), AF.Copy, scale=col(8, r0, r1))
                yield
                mm(pso.v(0, cw_), VNt.v(0, 128, r0, r1, r=True), AT.v(r0, r1, r0, r1, r=True), start=False, stop=False, inc=True, tp=tp)
                mm(PM[0].v(0, 128), kdec.v(0, 128, r0, r1, r=True), VNt.v(0, 128, r0, r1, r=True), tp=tp)
                yield
                stt(S.v(0, 128), S.v(0, 128), s.v(80 + 8 * ci_ + h, 80 + 8 * ci_ + h + 1), PM[0].v(0, 128), ALU.mult, ALU.add)
                cp("dve", Sb.v(0, 128), S.v(0, 128))
                cp("act", vn.v(t0 + r0, t0 + r1, r=True), pso.v(0, cw_))
                yield
        rms_gate_out(vn, ntok, params.v(PR_GNW, PR_GNW + 1), st["g"].v(0, ntok), oaT[h].v(0, ntok))

    def hgrn_pre(h, st, ntok, tls, ps_):
        slot = next_sec(("hgrn", h), lookahead=0)
        held.add(id(slot))
        qh, f_, logf, kp, t1, t2 = tmpFs[ps_]
        enb, bc, eb = f_, t2, t1
        a = proj_fm(slot, 0, ntok, xT)
        sigmoid_chain(t2.v(0, ntok), a.v(0, ntok), t1, t2, ntok)
        stt(qh.v(0, ntok), a.v(0, ntok), float(128.0 ** -0.5), t2.v(0, ntok), ALU.mult, ALU.mult)
        yield
        a = proj_fm(slot, 1, ntok, xT)
        sigmoid_chain(t2.v(0, ntok), a.v(0, ntok), t1, t2, ntok)
        ts("dve", f_.v(0, ntok), t2.v(0, ntok), dpar.v(8 + h, 9 + h), dpar.v(h, h + 1), ALU.mult, ALU.add)
        act(logf.v(0, ntok), f_.v(0, ntok), AF.Ln)
        ts("dve", kp.v(0, ntok), f_.v(0, ntok), -1.0, 1.0, ALU.mult, ALU.add)
        yield
        a = proj_fm(slot, 3, ntok, xT)
        sigmoid_chain(st["g"].v(0, ntok), a.v(0, ntok), t1, t2, ntok)
        yield
        scan(bc.v(0, ntok), cv(C_RST, C_RST + ntok), logf.v(0, ntok), 0.0, ALU.mult, ALU.add)
        act(eb.v(0, ntok), bc.v(0, ntok), AF.Exp)
        act(enb.v(0, ntok), bc.v(0, ntok), AF.Exp, scale=-1.0)
        yield
        tt("dve", st["q"].v(0, ntok, r=True), qh.v(0, ntok), eb.v(0, ntok), ALU.mult)
        tt("dve", st["k"].v(0, ntok, r=True), kp.v(0, ntok), enb.v(0, ntok), ALU.mult)
        if ntok >= HBLK:
            nb = ntok // HBLK
            cp("pool", V(st["e"], st["e"].h[:, 0:nb].rearrange("p (b c) -> p b c", c=1)),
               V(eb, eb.h[:, 0:nb * HBLK].rearrange("p (b c) -> p b c", c=HBLK)[:, :, HBLK - 1:HBLK]))
        else:
            cp("pool", st["e"].v(0, 1), eb.v(ntok - 1, ntok))
        yield
        for ti, (t0, nt) in enumerate(tls):
            a = next_acc()
            for kc in range(8):
                mm(a.v(0, 128, 0, nt), xT[kc].v(t0, t0 + nt), slot.v3(512, kc, kc + 1, 256, 384), start=(kc == 0), stop=(kc == 7))
            cp("act", st["v"].v(ti * 128, ti * 128 + 128, 0, nt, r=True), a.v(0, 128, 0, nt))
            if ti == len(tls) - 1:
                held.discard(id(slot))
            yield

    def hgrn_core(h, k, st, ntok, tls):
        qe, ke, ob_ = st["q"], st["k"], st["v"]
        SR, PM, pso = smR[k], ps_m[k], ps_o[k]
        S, Sb = Sh[h], Shb[h]
        aTm, ketm = SR[1], SR[2]
        for ti, (t0, nt) in enumerate(tls):
            vt = lambda a, b, r0=0, r1=128, r=False: st["v"].v(ti * 128 + a, ti * 128 + b, r0, r1, r)
            mm(PM[1].v(0, nt, 0, nt), ke.v(t0, t0 + nt, r=True), qe.v(t0, t0 + nt, r=True))
            tr(vb(PM[2], 0, 128, 0, nt), ke.v(t0, t0 + nt), identr.v(0, 128))
            yield
            tt("dve", aTm.v(0, nt, 0, nt, r=True), PM[1].v(0, nt, 0, nt), cv(C_BDH, C_BDH + nt, 0, nt), ALU.mult)
            yield
            cp("act", ketm.v(0, 128, 0, nt, r=True), vb(PM[2], 0, 128, 0, nt))
            yield
            blocks = [(b0, min(nt, b0 + HBLK)) for b0 in range(0, nt, HBLK)]
            for bi, (b0, b1) in enumerate(blocks):
                tp = (b0, 0) if nt > HBLK else None
                gb = (t0 + b0) // HBLK
                ebl = st["e"].v(gb, gb + 1)
                bw_ = b1 - b0
                memset("dve", pso.v(0, bw_), 0.0)
                mm(pso.v(0, bw_), vt(0, 128, b0, b1, r=True), aTm.v(b0, b1, b0, b1, r=True), start=False, stop=False, inc=True, tp=tp)
                mm(pso.v(0, bw_), Sb.v(0, 128), qe.v(t0 + b0, t0 + b1, r=True), start=False, stop=False, inc=True)
                mm(PM[3].v(0, 128), ketm.v(0, 128, b0, b1, r=True), vt(0, 128, b0, b1, r=True), tp=tp)
                yield
                ts("dve", S.v(0, 128), S.v(0, 128), ebl, None, ALU.mult)
                stt(S.v(0, 128), PM[3].v(0, 128), ebl, S.v(0, 128), ALU.mult, ALU.add)
                cp("act", Sb.v(0, 128), S.v(0, 128))
                cp("act", qe.v(t0 + b0, t0 + b1, r=True), pso.v(0, bw_))
                yield
        rms_gate_out(qe, ntok, params.v(PR_HNW, PR_HNW + 1), st["g"].v(0, ntok), obT[h].v(0, ntok))

    ngroups = (cfg.seq // G) * cfg.nseq + 1
    for _ in range(ngroups):
        plan_group()

    def store_states(conv_dst, gdn_dst, hgrn_dst):
        pg.dma("sp", conv_dst, halo.h[:, :], reads=[halo], semtile=halo, final=True)
        for h in range(NH):
            pg.dma("sp", gdn_dst[h], Sg[h].h[:, :], reads=[Sg[h]], semtile=Sg[h], final=True)
            pg.dma("sp", hgrn_dst[h], Sh[h].h[:, :], reads=[Sh[h]], semtile=Sh[h], final=True)

    def all_seqs():
        if cfg.phase <= 1:
            raise StopBuild()
        for sq_ in range(cfg.nseq):
            memset("pool", halo.v(0, 72), 0.0)
            for h in range(NH):
                memset("pool", Sg[h].v(0, 128), 0.0)
                memset("pool", Sh[h].v(0, 128), 0.0)
                memset("pool", Sgb[h].v(0, 128), 0.0)
                memset("pool", Shb[h].v(0, 128), 0.0)
            for g0 in range(0, cfg.seq, G):
                base = sq_ * cfg.seq + g0
                emit_group(xp[base:base + G, :], yp[base:base + G, :], G)
            store_states(convp[sq_], gdnp[sq_], hgrnp[sq_])
        pg.dma("sp", halo.h[:, :], convbuf, writes=[halo])
        for h in range(NH):
            pg.dma("sp", Sg[h].h[:, :], sgdn_in[h], writes=[Sg[h]])
            cp("pool", Sgb[h].v(0, 128), Sg[h].v(0, 128))
            pg.dma("sp", Sh[h].h[:, :], shgrn_in[h], writes=[Sh[h]])
            cp("pool", Shb[h].v(0, 128), Sh[h].v(0, 128))
        emit_group(xs, ys, NS)
        store_states(convs[0], gdns[0], hgrns[0])

    try:
        all_seqs()
    except StopBuild:
        pg.dma("sp", ys[0:16, :], x_tm[0].h[0:16, :], reads=[x_tm[0]], semtile=x_tm[0], final=True)
        nd = min(G, 512)
        if cfg.phase >= 3:
            src = oaT + obT if cfg.phase < 5 else mgT + mgT
            for i in range(16):
                stg = tmpF[i % 6]
                cp("dve", stg.v(0, nd), src[i].v(0, nd))
                pg.dma("sp", dbg[i, :, 0:nd], stg.h[:, 0:nd], reads=[stg], semtile=stg, final=True)
        if cfg.phase >= 5:
            pg.dma("sp", yp[0:128, :], x_tm[0].h[0:128, :], reads=[x_tm[0]], semtile=x_tm[0], final=True)

    pg.finish()
    pg.emit()
    return nc, pg


N_CORES = 8
_CACHE = {}


def _host_params(conv_w, hgrn_lb_logits, gdn_norm_w, hgrn_norm_w, a_log, dt_bias):
    p = np.zeros((128, NPAR), np.float32)
    cw = np.asarray(conv_w)[0]
    p[:, PR_CW:PR_CW + 96] = cw.reshape(4, 24, 128).transpose(2, 1, 0).reshape(128, 96)
    lg = np.asarray(hgrn_lb_logits)
    p[:, PR_L0:PR_L0 + 8] = lg[0].reshape(8, 128).T
    p[:, PR_L1:PR_L1 + 8] = lg[1].reshape(8, 128).T
    p[:, PR_GNW] = np.asarray(gdn_norm_w)[0]
    p[:, PR_HNW] = np.asarray(hgrn_norm_w)[0]
    p[:, PR_ALOG:PR_ALOG + 8] = np.asarray(a_log)[0][None, :]
    p[:, PR_DTB:PR_DTB + 8] = np.asarray(dt_bias)[0][None, :]
    return p


def run(cfg, x_prompt, x_sample, cache_gdn_conv, state_gdn, state_hgrn, w_in, conv_w, a_log, dt_bias,
        gdn_norm_w, hgrn_lb_logits, hgrn_norm_w, w_br_a, w_br_b, w_out, ln1_g, ln1_b, w_gate_up,
        w_down, ln2_g, ln2_b, n_cores=N_CORES):
    key = (cfg.nseq, cfg.seq, cfg.g, cfg.ns)
    if key not in _CACHE:
        _CACHE[key] = build(cfg)
    nc, _ = _CACHE[key]
    f = lambda a: np.ascontiguousarray(np.asarray(a, dtype=np.float32))
    params = _host_params(conv_w, hgrn_lb_logits, gdn_norm_w, hgrn_norm_w, a_log, dt_bias)
    lnp = np.concatenate([np.broadcast_to(f(v)[0][None, :], (128, D)) for v in (ln1_g, ln1_b, ln2_g, ln2_b)], axis=1)
    lnp = np.ascontiguousarray(lnp, dtype=np.float32)
    consts = make_consts()
    shared = {"w_in": f(w_in)[0], "w_br_a": f(w_br_a)[0], "w_br_b": f(w_br_b)[0], "w_out": f(w_out)[0],
              "w_gate_up": f(w_gate_up)[0], "w_down": f(w_down)[0], "params": params, "lnp": lnp, "consts": consts}
    xp = f(x_prompt)
    xs_ = f(x_sample)
    cb = f(cache_gdn_conv)[0]
    sg = f(state_gdn)[0]
    sh = f(state_hgrn)[0]
    in_maps = []
    for c in range(n_cores):
        m = dict(shared)
        m["xp"] = np.ascontiguousarray(xp[c * cfg.nseq:(c + 1) * cfg.nseq].reshape(cfg.nseq * cfg.seq, D))
        m["xs"] = np.ascontiguousarray(xs_[c])
        m["convbuf"] = np.ascontiguousarray(cb[c].reshape(3, 24, 128).transpose(2, 1, 0).reshape(128, 72))
        m["sgdn"] = np.ascontiguousarray(sg[c])
        m["shgrn"] = np.ascontiguousarray(sh[c])
        in_maps.append(m)
    res = run_bass_kernel_spmd(nc, in_maps, core_ids=list(range(n_cores)))
    R = res.results

    def conv_back(a):
        n = a.shape[0]
        return np.ascontiguousarray(a.reshape(n, 128, 24, 3).transpose(0, 3, 2, 1).reshape(n, 3, 3072))

    y_prompt = np.concatenate([r["yp"].reshape(cfg.nseq, cfg.seq, D) for r in R], axis=0)
    y_sample = np.stack([r["ys"] for r in R], axis=0)
    conv_p = np.concatenate([conv_back(r["convp"]) for r in R], axis=0)[None]
    gdn_p = np.concatenate([r["gdnp"] for r in R], axis=0)[None]
    hgrn_p = np.concatenate([r["hgrnp"] for r in R], axis=0)[None]
    conv_s = np.concatenate([conv_back(r["convs"]) for r in R], axis=0)[None]
    gdn_s = np.concatenate([r["gdns"] for r in R], axis=0)[None]
    hgrn_s = np.concatenate([r["hgrns"] for r in R], axis=0)[None]
    global LAST_DBG
    LAST_DBG = [r.get("dbg") for r in R]
    outs = (y_prompt, y_sample, conv_p, gdn_p, hgrn_p, conv_s, gdn_s, hgrn_s)
    return tuple(np.ascontiguousarray(o, dtype=np.float32) for o in outs)


def kernel(**inputs):
    cfg = Cfg(nseq=4, seq=2048, g=512, ns=16)
    return run(cfg, **inputs)
```

```python
import numpy as np
import concourse.bass as bass
import concourse.mybir as mybir
from concourse.bass_utils import run_bass_kernel_spmd

F32 = mybir.dt.float32
BF16 = mybir.dt.bfloat16
F32R = mybir.dt.float32r
AF = mybir.ActivationFunctionType
ALU = mybir.AluOpType

P = 128
D = 1024
NH = 8
DFF = 2816
NFF = 22
IN_TOTAL = 10256
OFF_QA, OFF_KA, OFF_VA, OFF_GA, OFF_AB = 0, 1024, 2048, 3072, 4096
OFF_QB, OFF_FB, OFF_IB, OFF_GB, OFF_MGA, OFF_MGB = 4112, 5136, 6160, 7184, 8208, 9232
ALPHA = 2.0 ** 0.25
LN_EPS = 1e-5
RMS_EPS = 1e-6
L2_EPS = 1e-6
GCH = 64
HBLK = 64
NEGBIG = -200.0

C_IDENT, C_SLT, C_TRI, C_NM, C_OFFD, C_CTRI, C_CREV, C_BDH, C_ONES, C_NBI = [i * 128 for i in range(10)]
C_CM = 1280
C_RST = 1282
NCONST = C_RST + 512
PR_CW, PR_L0, PR_L1, PR_GNW, PR_HNW, PR_ALOG, PR_DTB = 0, 96, 104, 112, 113, 114, 122
NPAR = 130


def make_consts():
    c = np.zeros((128, NCONST), np.float32)
    j = np.arange(128)[:, None]
    t = np.arange(128)[None, :]
    same_g = (j // GCH) == (t // GCH)
    same_h = (j // HBLK) == (t // HBLK)
    c[:, C_IDENT:C_IDENT + 128] = (j == t)
    c[:, C_SLT:C_SLT + 128] = (j > t)
    c[:, C_TRI:C_TRI + 128] = (j <= t)
    c[:, C_NM:C_NM + 128] = ~((t >= j) & same_g)
    c[:, C_OFFD:C_OFFD + 128] = (j != t)
    c[:, C_CTRI:C_CTRI + 128] = (j <= t) & same_g
    c[:, C_CREV:C_CREV + 128] = (j > t) & same_g
    c[:, C_BDH:C_BDH + 128] = (t >= j) & same_h
    c[:, C_ONES:C_ONES + 128] = 1.0
    c[:, C_NBI:C_NBI + 128] = NEGBIG * (j == t)
    c[:, C_CM] = (np.arange(128) < GCH)
    c[:, C_CM + 1] = (np.arange(128) >= GCH)
    c[:, C_RST:C_RST + 512] = ((np.arange(512) % HBLK) != 0)[None, :]
    return c


class Ev:
    __slots__ = ("sem", "val", "eng")

    def __init__(self, sem, val, eng):
        self.sem, self.val, self.eng = sem, val, eng


class T:
    def __init__(self, h, name, c0=0, hr=None):
        self.h, self.hr, self.name, self.c0 = h, hr, name, c0
        self.rclass = hr is not None
        self.w = None
        self.r = {}
        self.sem = None
        self.semcnt = 0
        self.const = False
        self.pend = 0
        self.bank = None

    def v(self, a=None, b=None, r0=0, r1=128, r=False):
        h = self.hr if r else self.h
        return V(self, h[r0:r1, self.c0 + a:self.c0 + b], r)

    def v3(self, inner, k0, k1, a, b, r0=0, r1=128, r=False):
        h = self.hr if r else self.h
        ap = h[r0:r1, :].rearrange("p (k c) -> p k c", c=inner)
        return V(self, ap[:, k0:k1, a:b], r)


class V:
    __slots__ = ("t", "ap", "r")

    def __init__(self, t, ap, r=False):
        self.t, self.ap, self.r = t, ap, r


def chk(eng, out):
    return


class Prog:
    ENG = ("pe", "act", "dve", "pool", "sp")

    def __init__(self, nc):
        self.nc = nc
        self.eobj = {"pe": nc.tensor, "act": nc.scalar, "dve": nc.vector, "pool": nc.gpsimd, "sp": nc.sync}
        self.stream = {e: [] for e in self.ENG}
        self.cnt = {e: 0 for e in self.ENG}
        self.esem = {e: nc.alloc_semaphore(f"prog_{e}") for e in self.ENG}
        self.known = {e: {} for e in self.ENG}
        self.pend = {e: [] for e in self.ENG}
        self.nsem = 5
        self.ninst = {e: 0 for e in self.ENG}
        self.final = []
        self.sym = {e: [] for e in self.ENG}

    def _waits(self, eng, reads, writes):
        need = {}

        def add(ev, raw):
            if ev is None:
                return
            if ev.eng == eng and not raw:
                return
            if self.known[eng].get(id(ev.sem), 0) >= ev.val:
                return
            k = id(ev.sem)
            if k not in need or need[k].val < ev.val:
                need[k] = ev

        for t in reads:
            assert t.pend == 0, f"read of tile {t.name} with unsignalled writer"
            add(t.w, True)
        for t in writes:
            add(t.w, False)
            for ev in t.r.values():
                add(ev, False)
        eo = self.eobj[eng]
        for ev in need.values():
            self.known[eng][id(ev.sem)] = ev.val
            self.stream[eng].append((lambda eo=eo, s=ev.sem, v=ev.val: eo.wait_ge(s, v)))
            self.sym[eng].append(("wait", id(ev.sem), ev.val))
            self.ninst[eng] += 1

    def _record(self, ev, reads, writes):
        for t in reads:
            if not t.const:
                k = id(ev.sem)
                if k not in t.r or t.r[k].val < ev.val:
                    t.r[k] = ev
        for t in writes:
            t.w = ev
            t.r = {}

    def op(self, eng, fn, reads=(), writes=(), inc=True):
        reads = [t for t in reads if t is not None]
        writes = [t for t in writes if t is not None]
        for t in list(reads) + list(writes):
            if t.bank is not None and t.bank not in writes:
                writes.append(t.bank)
        self._waits(eng, reads, writes)
        self.ninst[eng] += 1
        if inc:
            self.cnt[eng] += 1
            ev = Ev(self.esem[eng], self.cnt[eng], eng)
            sem = self.esem[eng]
            self.stream[eng].append(lambda: fn().then_inc(sem, 1))
            self.sym[eng].append(("inc", id(sem), 1))
            for (pr, pw) in self.pend[eng]:
                self._record(ev, pr, pw)
                for t in pw:
                    t.pend -= 1
            self.pend[eng] = []
            self._record(ev, reads, writes)
        else:
            self.stream[eng].append(fn)
            for t in writes:
                t.pend += 1
            self.pend[eng].append((reads, writes))

    def dma(self, q, out_ap, in_ap, reads=(), writes=(), semtile=None, final=False, **kw):
        nc = self.nc
        self._waits(q, list(reads), list(writes))
        st = semtile if semtile is not None else (writes[0] if writes else reads[0])
        if st.sem is None:
            st.sem = nc.alloc_semaphore(f"dsem{self.nsem}")
            self.nsem += 1
        st.semcnt += 16
        ev = Ev(st.sem, st.semcnt, "dma")
        eo = self.eobj[q]
        sem = st.sem
        self.stream[q].append(lambda: eo.dma_start(out=out_ap, in_=in_ap, **kw).then_inc(sem, 16))
        self.sym[q].append(("inc", id(sem), 16))
        self.ninst[q] += 1
        self._record(ev, list(reads), list(writes))
        if final:
            self.final.append(ev)

    def finish(self):
        eo = self.eobj["sp"]
        last = {}
        for ev in self.final:
            k = id(ev.sem)
            if k not in last or last[k].val < ev.val:
                last[k] = ev
        for ev in last.values():
            self.stream["sp"].append((lambda s=ev.sem, v=ev.val: eo.wait_ge(s, v)))
            self.sym["sp"].append(("wait", id(ev.sem), ev.val))

    def check(self):
        val = {}
        pc = {e: 0 for e in self.ENG}
        prog = True
        while prog:
            prog = False
            for e in self.ENG:
                st = self.sym[e]
                while pc[e] < len(st):
                    k, sid, v = st[pc[e]]
                    if k == "wait":
                        if val.get(sid, 0) < v:
                            break
                    else:
                        val[sid] = val.get(sid, 0) + v
                    pc[e] += 1
                    prog = True
        stuck = {e: (pc[e], len(self.sym[e]), self.sym[e][pc[e]], val.get(self.sym[e][pc[e]][1], 0))
                 for e in self.ENG if pc[e] < len(self.sym[e])}
        return stuck

    def emit(self):
        nc = self.nc
        with nc.Block() as block:
            @block.tensor
            def _(e):
                for f in self.stream["pe"]:
                    f()

            @block.scalar
            def _(e):
                for f in self.stream["act"]:
                    f()

            @block.vector
            def _(e):
                for f in self.stream["dve"]:
                    f()

            @block.gpsimd
            def _(e):
                for f in self.stream["pool"]:
                    f()

            @block.sync
            def _(e):
                for f in self.stream["sp"]:
                    f()


class Cfg:
    def __init__(self, nseq=4, seq=2048, g=512, ns=16, phase=99):
        self.nseq, self.seq, self.g, self.ns, self.phase = nseq, seq, g, ns, phase


class StopBuild(Exception):
    pass


def section_list():
    secs = []
    for h in range(NH):
        secs.append((("gdn", h), [("w_in", 0, 8, o + 128 * h) for o in (OFF_QA, OFF_KA, OFF_VA, OFF_GA)]))
    for h in range(NH):
        secs.append((("hgrn", h), [("w_in", 0, 8, o + 128 * h) for o in (OFF_QB, OFF_FB, OFF_IB, OFF_GB)]))
    for c in range(8):
        secs.append((("mrg", c), [("w_in", 0, 8, OFF_MGA + 128 * c), ("w_in", 0, 8, OFF_MGB + 128 * c),
                                  ("w_br_a", 0, 8, 128 * c), ("w_br_b", 0, 8, 128 * c)]))
    for ch in range(2):
        secs.append((("wout", ch), [("w_out", 0, 8, 512 * ch + 128 * p) for p in range(4)]))
    for s in range(NFF // 2):
        secs.append((("up", s), [("w_gate_up", 0, 8, 128 * (2 * s)), ("w_gate_up", 0, 8, DFF + 128 * (2 * s)),
                                 ("w_gate_up", 0, 8, 128 * (2 * s + 1)), ("w_gate_up", 0, 8, DFF + 128 * (2 * s + 1))]))
    for ch in range(2):
        for r in range(3):
            nk = 8 if r < 2 else NFF - 16
            secs.append((("down", ch, r), [("w_down", 1024 * r, nk, 512 * ch + 128 * p) for p in range(4)]))
    return secs


def build(cfg):
    nc = bass.Bass("TRN2", target_bir_lowering=False)
    G = cfg.g
    NT = cfg.nseq * cfg.seq
    NS = cfg.ns

    def din(name, shape):
        return nc.dram_tensor(name, list(shape), F32, kind="ExternalInput").ap()

    def dout(name, shape):
        return nc.dram_tensor(name, list(shape), F32, kind="ExternalOutput").ap()

    xp = din("xp", (NT, D))
    xs = din("xs", (NS, D))
    convbuf = din("convbuf", (P, 72))
    sgdn_in = din("sgdn", (NH, P, P))
    shgrn_in = din("shgrn", (NH, P, P))
    W = {
        "w_in": din("w_in", (D, IN_TOTAL)),
        "w_br_a": din("w_br_a", (D, D)),
        "w_br_b": din("w_br_b", (D, D)),
        "w_out": din("w_out", (D, D)),
        "w_gate_up": din("w_gate_up", (D, 2 * DFF)),
        "w_down": din("w_down", (DFF, D)),
    }
    params_d = din("params", (P, NPAR))
    lnp_d = din("lnp", (P, 4 * D))
    consts_d = din("consts", (P, NCONST))

    yp = dout("yp", (NT, D))
    ys = dout("ys", (NS, D))
    convp = dout("convp", (cfg.nseq, P, 72))
    gdnp = dout("gdnp", (cfg.nseq, NH, P, P))
    hgrnp = dout("hgrnp", (cfg.nseq, NH, P, P))
    convs = dout("convs", (1, P, 72))
    gdns = dout("gdns", (1, NH, P, P))
    hgrns = dout("hgrns", (1, NH, P, P))

    dbg = dout("dbg", (16, P, 512)) if cfg.phase < 99 else None
    secs = section_list()
    NSEC = len(secs)
    wscr = nc.dram_tensor("wscr", [NSEC, P, 4096], BF16).ap()

    pg = Prog(nc)
    _n = [0]

    def sb(shape_w, dt=F32, name=None, r=False):
        _n[0] += 1
        nm = f"{name or 'sb'}{_n[0]}"
        if r:
            dt = BF16
        h = nc.alloc_sbuf_tensor(nm, [P, shape_w], dt)
        t_ = T(h, nm, 0, h if r else None)
        t_.rclass = False
        return t_

    banks = [nc.alloc_psum_tensor(f"bank{i}", [P, 512], F32) for i in range(8)]
    block_ = [T(None, f"banklock{i}") for i in range(8)]

    banks_b = [b_.bitcast(BF16) for b_ in banks]

    def pst(bi, name, c0=0):
        t = T(banks[bi], name, c0)
        t.bank = block_[bi]
        t.hb = banks_b[bi]
        return t

    def vb(t, a, b, r0=0, r1=128):
        return V(t, t.hb[r0:r1, 2 * t.c0 + a:2 * t.c0 + b])

    acc = [pst(i, f"acc{i}") for i in range(3)]
    acc_pools = {"dense": [0, 1, 2], "pre": [0, 1], "rms": [2]}
    acc_i = {"dense": 0, "pre": 0, "rms": 0}
    acc_mode = ["dense"]

    def next_acc(pool=None):
        pool = pool or acc_mode[0]
        lst = acc_pools[pool]
        t = acc[lst[acc_i[pool] % len(lst)]]
        acc_i[pool] += 1
        return t

    NCORE = 4
    ps_o = [pst(3, f"pso{k}", 128 * k) for k in range(NCORE)]
    ps_m = [[pst(4 + k, f"psm{k}_{q}", 128 * q) for q in range(4)] for k in range(NCORE)]

    consts = sb(NCONST, name="consts")
    params = sb(NPAR, name="params")
    lnp = sb(2 * D, name="lnp")
    identr = sb(128, name="identr", r=True)
    onesr = sb(128, name="onesr", r=True)
    pg.dma("sp", consts.h[:, :], consts_d, writes=[consts])
    pg.dma("sp", params.h[:, :], params_d, writes=[params])

    def rd(*vs):
        return [v.t for v in vs if isinstance(v, V)]

    def apx(x):
        return x.ap if isinstance(x, V) else x

    def mm(out, lhsT, rhs, start=True, stop=True, inc=None, tp=None):
        kw = {}
        if tp is not None:
            kw["tile_position"] = tp
        pg.op("pe", lambda: nc.tensor.matmul(out.ap, lhsT.ap, rhs.ap, start=start, stop=stop, **kw),
              reads=rd(lhsT, rhs), writes=[out.t], inc=(stop if inc is None else inc))

    def tr(out, in_, ident):
        pg.op("pe", lambda: nc.tensor.transpose(out.ap, in_.ap, ident.ap), reads=rd(in_, ident), writes=[out.t])

    def act(out, in_, func, bias=0.0, scale=1.0, accum=None):
        kw = {}
        if accum is not None:
            kw["accum_out"] = accum.ap
        wr = [out.t] + ([accum.t] if accum is not None else [])
        chk("act", out)
        pg.op("act", lambda: nc.scalar.activation(out=out.ap, in_=in_.ap, func=func, bias=apx(bias), scale=apx(scale), **kw),
              reads=rd(in_, bias, scale), writes=wr)

    def ts(eng, out, in0, s1, s2, op0, op1=None):
        eo = pg.eobj[eng]
        chk(eng, out)
        if op1 is None:
            pg.op(eng, lambda: eo.tensor_scalar(out=out.ap, in0=in0.ap, scalar1=apx(s1), scalar2=None, op0=op0),
                  reads=rd(in0, s1), writes=[out.t])
        else:
            pg.op(eng, lambda: eo.tensor_scalar(out=out.ap, in0=in0.ap, scalar1=apx(s1), scalar2=apx(s2), op0=op0, op1=op1),
                  reads=rd(in0, s1, s2), writes=[out.t])

    def tt(eng, out, a, b, op):
        eo = pg.eobj[eng]
        chk(eng, out)
        pg.op(eng, lambda: eo.tensor_tensor(out=out.ap, in0=a.ap, in1=b.ap, op=op), reads=rd(a, b), writes=[out.t])

    def stt(out, in0, scalar, in1, op0, op1):
        chk("dve", out)
        pg.op("dve", lambda: nc.vector.scalar_tensor_tensor(out=out.ap, in0=in0.ap, scalar=apx(scalar), in1=in1.ap, op0=op0, op1=op1),
              reads=rd(in0, scalar, in1), writes=[out.t])

    def scan(out, d0, d1, init, op0, op1):
        pg.op("dve", lambda: nc.vector.tensor_tensor_scan(out=out.ap, data0=d0.ap, data1=d1.ap, initial=init, op0=op0, op1=op1),
              reads=rd(d0, d1), writes=[out.t])

    def recip(out, in_):
        pg.op("dve", lambda: nc.vector.reciprocal(out=out.ap, in_=in_.ap), reads=rd(in_), writes=[out.t])

    def cp(eng, out, in_):
        if eng == "act":
            act(out, in_, AF.Copy)
        else:
            eo = pg.eobj[eng]
            chk(eng, out)
            pg.op(eng, lambda: eo.tensor_copy(out=out.ap, in_=in_.ap), reads=rd(in_), writes=[out.t])

    def memset(eng, out, val):
        eo = pg.eobj[eng]
        chk(eng, out)
        pg.op(eng, lambda: eo.memset(out.ap, val), writes=[out.t])

    def sigmoid_chain(out, in_, tmp1, tmp2, n, rows=128):
        act(tmp1.v(0, n, 0, rows), in_, AF.Exp, scale=-1.0)
        act(tmp2.v(0, n, 0, rows), tmp1.v(0, n, 0, rows), AF.Ln, bias=1.0)
        act(out, tmp2.v(0, n, 0, rows), AF.Exp, scale=-1.0)

    cv = lambda a, b, r0=0, r1=128: consts.v(a, b, r0, r1)

    cp("dve", identr.v(0, 128, r=True), cv(C_IDENT, C_IDENT + 128))
    cp("dve", onesr.v(0, 128, r=True), cv(C_ONES, C_ONES + 128))

    dpar = sb(32, name="dpar")
    tt("dve", dpar.v(24, 32), params.v(PR_L1, PR_L1 + 8), params.v(PR_L0, PR_L0 + 8), ALU.subtract)
    act(dpar.v(24, 32), dpar.v(24, 32), AF.Exp)
    ts("dve", dpar.v(24, 32), dpar.v(24, 32), 1.0, None, ALU.add)
    recip(dpar.v(0, 8), dpar.v(24, 32))
    ts("dve", dpar.v(8, 16), dpar.v(0, 8), -1.0, 1.0, ALU.mult, ALU.add)
    act(dpar.v(16, 24), params.v(PR_ALOG, PR_ALOG + 8), AF.Exp)
    ts("dve", dpar.v(16, 24), dpar.v(16, 24), -1.0, None, ALU.mult)
    for t_ in (consts, params, identr, onesr):
        t_.const = True

    x_tm = [sb(D, name="xtm") for _ in range(max(1, G // 128))]
    NXT = len(x_tm)
    xT = [sb(G, BF16, name="xT") for _ in range(8)]
    NRING = 3
    wring = [sb(4096, BF16, name="wring") for _ in range(NRING)]
    wab = sb(128, BF16, name="wab")
    tmpFs = [[sb(G + 4, name="tmpF") for _ in range(6)] for _ in range(2)]
    tmpRs = [sb(G + 4, name="tmpR", r=True) for _ in range(2)]
    ubs = [[sb(G + 4, name="ub", r=True) for _ in range(3)] for _ in range(2)]
    dgs = [[sb(128, name="dg", r=True) for _ in range(4)] for _ in range(2)]
    tmpF = tmpFs[0]
    tmpR = tmpRs[0]
    store = [[dict(q=sb(G + 4, name="stq", r=True), k=sb(G + 4, name="stk", r=True),
                   v=sb(G + 4, name="stv", r=True), g=sb(G + 4, BF16, name="stg"), e=sb(16, name="ebl")) for k_ in range(NCORE)]
             for _ in range(2)]
    rmsR = sb(G + 4, name="rmsR", r=True)
    rms1 = sb(G + 4, name="rms1")
    smF = [[sb(128, name="smF") for _ in range(4)] for _ in range(NCORE)]
    smR = [[sb(128, name="smR", r=True) for _ in range(7)] for _ in range(NCORE)]
    tsc = [sb(96, name="tsc") for _ in range(NXT)]
    Sg = [sb(128, name="Sg") for _ in range(NH)]
    Sh = [sb(128, name="Sh") for _ in range(NH)]
    Sgb = [sb(128, name="Sgb", r=True) for _ in range(NH)]
    Shb = [sb(128, name="Shb", r=True) for _ in range(NH)]
    halo = sb(72, name="halo")
    sstage = sb(128, name="sstage")
    oaT = [sb(G, BF16, name="oaT") for _ in range(8)]
    obT = [sb(G, BF16, name="obT") for _ in range(8)]
    mgT = [sb(G, BF16, name="mgT") for _ in range(8)]
    actT = oaT + obT + mgT[:6]
    lnsc = [sb(16, name="lnsc") for _ in range(NXT)]
    lnjunk = sb(D, BF16, name="lnjunk")
    dtile = {n: T(None, "dram_" + str(n)) for n in range(NSEC)}

    cast_engs = ["dve", "pool", "act"]
    ci = 0
    stg_i = 0
    st = x_tm[stg_i % NXT]
    stg_i += 1
    pg.dma("sp", st.h[:, 0:128].rearrange("p (k c) -> p k c", c=16),
           W["w_in"][:, OFF_AB:OFF_AB + 16].rearrange("(k p) c -> p k c", p=128), writes=[st])
    cp("dve", wab.v(0, 128), st.v(0, 128))
    wab.const = True
    for si, (nm, pieces) in enumerate(secs):
        slot = wring[si % NRING]
        for pi, (wn, r0, nk, c0) in enumerate(pieces):
            st = x_tm[stg_i % NXT]
            stg_i += 1
            pg.dma("sp", st.h[:, 0:nk * 128].rearrange("p (k c) -> p k c", c=128),
                   W[wn][r0:r0 + nk * 128, c0:c0 + 128].rearrange("(k p) c -> p k c", p=128), writes=[st])
            cp(cast_engs[ci % 3], slot.v3(512, 0, nk, pi * 128, pi * 128 + 128), st.v3(128, 0, nk, 0, 128))
            ci += 1
        nk = pieces[0][2]
        pg.dma("pool", wscr[si, :, 0:nk * 512], slot.h[:, 0:nk * 512], reads=[slot], writes=[dtile[si]], semtile=dtile[si])

    sec_idx = {nm: i for i, (nm, _) in enumerate(secs)}
    use_seq = []
    loaded = [0]
    slot_of = {}

    def plan_group():
        for (nm, _) in secs:
            use_seq.append(sec_idx[nm])

    def ensure(k):
        while loaded[0] <= k and loaded[0] < len(use_seq):
            u = loaded[0]
            si = use_seq[u]
            slot = wring[u % NRING]
            assert id(slot) not in held, "weight ring slot still held by a pre stream"
            nk = secs[si][1][0][2]
            pg.dma("sp", slot.h[:, 0:nk * 512], wscr[si, :, 0:nk * 512], reads=[dtile[si]], writes=[slot])
            slot_of[u] = slot
            loaded[0] += 1

    use_ptr = [0]
    held = set()

    def next_sec(expect, lookahead=NRING - 1):
        u = use_ptr[0]
        assert secs[use_seq[u]][0] == expect, (secs[use_seq[u]][0], expect)
        ensure(min(u + lookahead, len(use_seq) - 1))
        use_ptr[0] += 1
        return slot_of[u]

    def next_secs(expects):
        u = use_ptr[0]
        ensure(min(u + NRING - 1, len(use_seq) - 1))
        out = []
        for e in expects:
            assert secs[use_seq[use_ptr[0]]][0] == e
            out.append(slot_of[use_ptr[0]])
            use_ptr[0] += 1
        return out

    def tiles_of(ntok):
        return [(t0, min(128, ntok - t0)) for t0 in range(0, ntok, 128)]

    def proj_fm(slot, piece, ntok, src, nk=8):
        a = next_acc()
        for kc in range(nk):
            mm(a.v(0, ntok), slot.v3(512, kc, kc + 1, piece * 128, piece * 128 + 128), src[kc].v(0, ntok),
               start=(kc == 0), stop=(kc == nk - 1))
        return a

    def rms_gate_out(osb, ntok, normw_col, gate_v, dst):
        sq, t1, t2 = rmsR, rms1, rms1
        act(sq.v(0, ntok, r=True), osb.v(0, ntok), AF.Square)
        a = next_acc("rms")
        mm(a.v(0, ntok), onesr.v(0, 128, r=True), sq.v(0, ntok, r=True))
        act(t1.v(0, ntok), a.v(0, ntok), AF.Ln, bias=RMS_EPS, scale=1.0 / 128)
        act(t1.v(0, ntok), t1.v(0, ntok), AF.Exp, scale=-0.5)
        tt("dve", t2.v(0, ntok), osb.v(0, ntok), t1.v(0, ntok), ALU.mult)
        stt(dst, t2.v(0, ntok), normw_col, gate_v, ALU.mult, ALU.mult)

    def run_cores(gens):
        gens = list(gens)
        while gens:
            for g_ in list(gens):
                try:
                    next(g_)
                except StopIteration:
                    gens.remove(g_)

    def emit_group(x_src, y_dst, ntok):
        tls = tiles_of(ntok)
        for ti, (t0, nt) in enumerate(tls):
            pg.dma("sp", x_tm[ti].h[0:nt, :], x_src[t0:t0 + nt, :], writes=[x_tm[ti]])
        for kc in range(8):
            a = next_acc()
            for ti, (t0, nt) in enumerate(tls):
                tr(a.v(t0, t0 + nt), x_tm[ti].v(kc * 128, kc * 128 + 128, 0, nt), cv(C_IDENT, C_IDENT + nt, 0, nt))
            cp("act" if kc % 2 else "dve", xT[kc].v(0, ntok), a.v(0, ntok))
        def stage_b():
            for ti, (t0, nt) in enumerate(tls):
                s = tsc[ti]
                a = ps_m[0][0]
                for kc in range(8):
                    mm(a.v(0, 16, 0, nt), xT[kc].v(t0, t0 + nt), wab.v(kc * 16, kc * 16 + 16), start=(kc == 0), stop=(kc == 7))
                tt("dve", s.v(64, 72, 0, nt), a.v(0, 8, 0, nt), params.v(PR_DTB, PR_DTB + 8, 0, nt), ALU.add)
                act(s.v(64, 72, 0, nt), s.v(64, 72, 0, nt), AF.Exp)
                act(s.v(64, 72, 0, nt), s.v(64, 72, 0, nt), AF.Ln, bias=1.0)
                tt("dve", s.v(0, 8, 0, nt), s.v(64, 72, 0, nt), dpar.v(16, 24, 0, nt), ALU.mult)
                act(s.v(72, 80, 0, nt), a.v(8, 16, 0, nt), AF.Exp, scale=-1.0)
                ts("dve", s.v(72, 80, 0, nt), s.v(72, 80, 0, nt), 1.0, None, ALU.add)
                recip(s.v(8, 16, 0, nt), s.v(72, 80, 0, nt))
                ts("dve", s.v(16, 24, 0, nt), s.v(8, 16, 0, nt), -1.0, None, ALU.mult)
                yield
                b = ps_m[0][1]
                mm(b.v(0, 8, 0, nt), cv(C_CTRI, C_CTRI + nt, 0, nt), s.v(0, 8, 0, nt))
                act(s.v(24, 32, 0, nt), b.v(0, 8, 0, nt), AF.Exp)
                ts("dve", s.v(32, 40, 0, nt), s.v(24, 32, 0, nt), -1.0, None, ALU.mult)
                c = ps_m[0][2]
                mm(c.v(0, 8, 0, nt), cv(C_CREV, C_CREV + nt, 0, nt), s.v(0, 8, 0, nt))
                act(s.v(40, 48, 0, nt), c.v(0, 8, 0, nt), AF.Exp)
                for ch in range(2):
                    ts("dve", s.v(48 + 8 * ch, 56 + 8 * ch, 0, nt), s.v(0, 8, 0, nt), cv(C_CM + ch, C_CM + ch + 1, 0, nt), None, ALU.mult)
                d = ps_m[0][3]
                mm(d.v(0, 16), cv(C_ONES, C_ONES + 128, 0, nt), s.v(48, 64, 0, nt))
                act(s.v(80, 96), d.v(0, 16), AF.Exp)
                yield

        if cfg.phase <= 2.0:
            raise StopBuild()
        tasks = [("g", h) for h in range(NH)] + [("h", h) for h in range(NH)]
        nw = len(tasks) // NCORE

        free_store = [store[p_][k_] for p_ in range(2) for k_ in range(NCORE)]
        free_core = list(range(NCORE))
        ready = []
        pre_act = [None, None]
        core_act = []
        nxt = [0]

        def mk_pre(task, st, ps_):
            kind, h = task
            return gdn_pre(h, st, ntok, ps_) if kind == "g" else hgrn_pre(h, st, ntok, tls, ps_)

        def mk_core(task, k, st):
            kind, h = task
            return gdn_core(h, k, st, ntok, tls) if kind == "g" else hgrn_core(h, k, st, ntok, tls)

        acc_mode[0] = "pre"
        sb_gen = [stage_b()]
        rnd_ = [0]
        while True:
            if sb_gen[0] is not None:
                try:
                    next(sb_gen[0])
                except StopIteration:
                    sb_gen[0] = None
            for ps_ in range(2):
                if pre_act[ps_] is None and nxt[0] < len(tasks) and free_store:
                    st = free_store.pop(0)
                    task = tasks[nxt[0]]
                    nxt[0] += 1
                    pre_act[ps_] = (mk_pre(task, st, ps_), task, st)
            while ready and free_core and sb_gen[0] is None:
                k = free_core.pop(0)
                task, st = ready.pop(0)
                core_act.append((mk_core(task, k, st), k, st))
            if not core_act and pre_act[0] is None and pre_act[1] is None and sb_gen[0] is None:
                assert nxt[0] == len(tasks) and not ready
                break
            for item in list(core_act):
                g_, k, st = item
                try:
                    next(g_)
                except StopIteration:
                    core_act.remove(item)
                    free_core.append(k)
                    free_store.append(st)
            rnd_[0] += 1
            throttle = len(core_act) == NCORE and len(ready) >= 1 and (rnd_[0] % 2 == 1)
            for ps_ in range(2):
                if pre_act[ps_] is not None and not throttle:
                    g_, task, st = pre_act[ps_]
                    try:
                        next(g_)
                    except StopIteration:
                        ready.append((task, st))
                        pre_act[ps_] = None
        acc_mode[0] = "dense"
        if cfg.phase <= 4:
            raise StopBuild()
        B = tmpF
        for c in range(8):
            slot = next_sec(("mrg", c))
            p1 = proj_fm(slot, 0, ntok, xT)
            sigmoid_chain(B[0].v(0, ntok), p1.v(0, ntok), B[0], B[0], ntok)
            p2 = proj_fm(slot, 1, ntok, xT)
            sigmoid_chain(B[1].v(0, ntok), p2.v(0, ntok), B[1], B[1], ntok)
            p3 = proj_fm(slot, 2, ntok, oaT)
            tt("dve", B[2].v(0, ntok), p3.v(0, ntok), B[0].v(0, ntok), ALU.mult)
            p4 = proj_fm(slot, 3, ntok, obT)
            tt("dve", B[3].v(0, ntok), p4.v(0, ntok), B[1].v(0, ntok), ALU.mult)
            tt("dve", mgT[c].v(0, ntok), B[2].v(0, ntok), B[3].v(0, ntok), ALU.add)
        for ch in range(2):
            slot = next_sec(("wout", ch))
            for ti, (t0, nt) in enumerate(tls):
                a = next_acc()
                for kc in range(8):
                    mm(a.v(0, 512, 0, nt), mgT[kc].v(t0, t0 + nt), slot.v3(512, kc, kc + 1, 0, 512), start=(kc == 0), stop=(kc == 7))
                stt(x_tm[ti].v(ch * 512, ch * 512 + 512, 0, nt), x_tm[ti].v(ch * 512, ch * 512 + 512, 0, nt), ALPHA,
                    a.v(0, 512, 0, nt), ALU.mult, ALU.add)
        pg.dma("sp", lnp.h[:, :], lnp_d[:, 0:2 * D], writes=[lnp])
        for ti, (t0, nt) in enumerate(tls):
            layer_norm(x_tm[ti], nt, lnsc[ti], 0, None)
        if cfg.phase <= 5:
            raise StopBuild()
        for kc in range(8):
            a = next_acc()
            for ti, (t0, nt) in enumerate(tls):
                tr(a.v(t0, t0 + nt), x_tm[ti].v(kc * 128, kc * 128 + 128, 0, nt), cv(C_IDENT, C_IDENT + nt, 0, nt))
            cp("act" if kc % 2 else "dve", xT[kc].v(0, ntok), a.v(0, ntok))
        for s_ in range(NFF // 2):
            slot = next_sec(("up", s_))
            for jj in range(2):
                j = 2 * s_ + jj
                Bq = tmpF[3 * jj:3 * jj + 3]
                pg_ = proj_fm(slot, 2 * jj, ntok, xT)
                pu_ = proj_fm(slot, 2 * jj + 1, ntok, xT)
                sigmoid_chain(Bq[0].v(0, ntok), pg_.v(0, ntok), Bq[0], Bq[0], ntok)
                tt("dve", Bq[1].v(0, ntok), pg_.v(0, ntok), Bq[0].v(0, ntok), ALU.mult)
                tt("dve", actT[j].v(0, ntok), pu_.v(0, ntok), Bq[1].v(0, ntok), ALU.mult)
        for ch in range(2):
            slots = next_secs([("down", ch, r) for r in range(3)])
            for ti, (t0, nt) in enumerate(tls):
                a = next_acc()
                for j in range(NFF):
                    mm(a.v(0, 512, 0, nt), actT[j].v(t0, t0 + nt), slots[j // 8].v3(512, j % 8, j % 8 + 1, 0, 512),
                       start=(j == 0), stop=(j == NFF - 1))
                stt(x_tm[ti].v(ch * 512, ch * 512 + 512, 0, nt), x_tm[ti].v(ch * 512, ch * 512 + 512, 0, nt), ALPHA,
                    a.v(0, 512, 0, nt), ALU.mult, ALU.add)
        pg.dma("sp", lnp.h[:, :], lnp_d[:, 2 * D:4 * D], writes=[lnp])
        for ti, (t0, nt) in enumerate(tls):
            layer_norm(x_tm[ti], nt, lnsc[ti], 2, None)
            pg.dma("sp", y_dst[t0:t0 + nt, :], x_tm[ti].h[0:nt, :], reads=[x_tm[ti]], semtile=x_tm[ti], final=True)

    def layer_norm(xt, nt, sc, which, B):
        junk = lnjunk
        act(junk.v(0, D, 0, nt), xt.v(0, D, 0, nt), AF.Copy, accum=sc.v(0, 1, 0, nt))
        act(junk.v(0, D, 0, nt), xt.v(0, D, 0, nt), AF.Square, accum=sc.v(1, 2, 0, nt))
        ts("dve", sc.v(8, 9, 0, nt), sc.v(0, 1, 0, nt), 1.0 / D, None, ALU.mult)
        tt("dve", sc.v(9, 10, 0, nt), sc.v(8, 9, 0, nt), sc.v(8, 9, 0, nt), ALU.mult)
        stt(sc.v(10, 11, 0, nt), sc.v(1, 2, 0, nt), 1.0 / D, sc.v(9, 10, 0, nt), ALU.mult, ALU.subtract)
        act(sc.v(11, 12, 0, nt), sc.v(10, 11, 0, nt), AF.Ln, bias=LN_EPS)
        act(sc.v(11, 12, 0, nt), sc.v(11, 12, 0, nt), AF.Exp, scale=-0.5)
        stt(sc.v(12, 13, 0, nt), sc.v(8, 9, 0, nt), -1.0, sc.v(11, 12, 0, nt), ALU.mult, ALU.mult)
        act(xt.v(0, D, 0, nt), xt.v(0, D, 0, nt), AF.Identity, bias=sc.v(12, 13, 0, nt), scale=sc.v(11, 12, 0, nt))
        tt("dve", xt.v(0, D, 0, nt), xt.v(0, D, 0, nt), lnp.v(0, D, 0, nt), ALU.mult)
        tt("dve", xt.v(0, D, 0, nt), xt.v(0, D, 0, nt), lnp.v(D, 2 * D, 0, nt), ALU.add)

    def silu_from(out, y, t1, t2, n):
        sigmoid_chain(t2.v(0, n), y, t1, t2, n)
        tt("dve", out, y, t2.v(0, n), ALU.mult)

    def gdn_pre(h, st, ntok, ps_):
        slot = next_sec(("gdn", h), lookahead=0)
        held.add(id(slot))
        TF, TR, ub, dg = tmpFs[ps_], tmpRs[ps_], ubs[ps_], dgs[ps_]
        t1, t2, cq, ck = TF[0], TF[1], TF[2], TF[3]
        for i in range(3):
            a = proj_fm(slot, i, ntok, xT)
            ch = i * 8 + h
            cp("pool", ub[i].v(0, 3), halo.v(ch * 3, ch * 3 + 3))
            cp("act", ub[i].v(3, 3 + ntok), a.v(0, ntok))
            cp("dve", halo.v(ch * 3, ch * 3 + 3), a.v(ntok - 3, ntok))
            yield
        a = proj_fm(slot, 3, ntok, xT)
        held.discard(id(slot))
        sigmoid_chain(t2.v(0, ntok), a.v(0, ntok), t1, t2, ntok)
        tt("dve", st["g"].v(0, ntok), a.v(0, ntok), t2.v(0, ntok), ALU.mult)
        yield
        cvo = [cq, ck, st["v"]]
        for i in range(3):
            ch = i * 8 + h
            for j in range(4):
                ts("dve", dg[j].v(0, 128), identr.v(0, 128), params.v(PR_CW + ch * 4 + j, PR_CW + ch * 4 + j + 1), None, ALU.mult)
            yield
            a = next_acc()
            for j in range(4):
                mm(a.v(0, ntok), dg[j].v(0, 128), ub[i].v(j, j + ntok), start=(j == 0), stop=(j == 3))
            sigmoid_chain(t2.v(0, ntok), a.v(0, ntok), t1, t2, ntok)
            tt("dve", cvo[i].v(0, ntok), a.v(0, ntok), t2.v(0, ntok), ALU.mult)
            yield
        for src, dst, lb_ in ((cq, st["q"], -0.5 * np.log(128.0)), (ck, st["k"], 0.0)):
            act(TR.v(0, ntok, r=True), src.v(0, ntok), AF.Square)
            a = next_acc()
            mm(a.v(0, ntok), onesr.v(0, 128, r=True), TR.v(0, ntok, r=True))
            act(t1.v(0, ntok), a.v(0, ntok), AF.Ln, bias=L2_EPS)
            act(t1.v(0, ntok), t1.v(0, ntok), AF.Exp, scale=-0.5, bias=float(lb_))
            tt("dve", dst.v(0, ntok, r=True), src.v(0, ntok), t1.v(0, ntok), ALU.mult)
            yield

    def gdn_core(h, k, st, ntok, tls):
        qn, kn, vn = st["q"], st["k"], st["v"]
        SF, SR, PM, pso = smF[k], smR[k], ps_m[k], ps_o[k]
        S, Sb = Sg[h], Sgb[h]
        for ti, (t0, nt) in enumerate(tls):
            s = tsc[ti]
            col = lambda base, r0=0, r1=nt: s.v(base + h, base + h + 1, r0, r1)
            E, Es, lg, vtm = SF[0], SF[1], SF[2], SF[3]
            AT, Q, QT, X = SR[0], [SR[1], SR[2]], [SR[3], SR[4]], [SR[5], SR[6]]
            kdec, Wt, VNt, P1s = SR[1], SR[2], SR[3], SR[4]
            mm(PM[0].v(0, nt, 0, nt), kn.v(t0, t0 + nt, r=True), kn.v(t0, t0 + nt, r=True))
            mm(PM[1].v(0, nt, 0, nt), kn.v(t0, t0 + nt, r=True), qn.v(t0, t0 + nt, r=True))
            ts("dve", lg.v(0, nt, 0, nt), cv(C_SLT, C_SLT + nt, 0, nt), col(0), None, ALU.mult)
            yield
            mm(PM[2].v(0, nt, 0, nt), lg.v(0, nt, 0, nt), cv(C_TRI, C_TRI + nt, 0, nt), start=True, stop=False)
            mm(PM[2].v(0, nt, 0, nt), cv(C_NBI, C_NBI + nt, 0, nt), cv(C_NM, C_NM + nt, 0, nt), start=False, stop=True)
            act(E.v(0, nt, 0, nt), PM[2].v(0, nt, 0, nt), AF.Exp)
            yield
            tt("dve", Es.v(0, nt, 0, nt), E.v(0, nt, 0, nt), cv(C_OFFD, C_OFFD + nt, 0, nt), ALU.mult)
            tt("dve", AT.v(0, nt, 0, nt, r=True), PM[1].v(0, nt, 0, nt), E.v(0, nt, 0, nt), ALU.mult)
            yield
            stt(Q[0].v(0, nt, 0, nt, r=True), PM[0].v(0, nt, 0, nt), col(16), Es.v(0, nt, 0, nt), ALU.mult, ALU.mult)
            yield
            tr(vb(PM[3], 0, nt, 0, nt), Q[0].v(0, nt, 0, nt), identr.v(0, nt, 0, nt))
            tt("dve", X[0].v(0, nt, 0, nt, r=True), Q[0].v(0, nt, 0, nt), cv(C_IDENT, C_IDENT + nt, 0, nt), ALU.add)
            yield
            cp("act", QT[0].v(0, nt, 0, nt, r=True), vb(PM[3], 0, nt, 0, nt))
            yield
            cur = 0
            for lvl in range(5):
                nx = 1 - cur
                if lvl < 4:
                    mm(PM[0].v(0, nt, 0, nt), QT[cur].v(0, nt, 0, nt, r=True), Q[cur].v(0, nt, 0, nt, r=True))
                mm(PM[1].v(0, nt, 0, nt), Q[cur].v(0, nt, 0, nt, r=True), QT[cur].v(0, nt, 0, nt, r=True))
                yield
                if lvl < 4:
                    cp("act", Q[nx].v(0, nt, 0, nt, r=True), PM[0].v(0, nt, 0, nt))
                cp("dve", QT[nx].v(0, nt, 0, nt, r=True), PM[1].v(0, nt, 0, nt))
                yield
                mm(PM[2].v(0, nt, 0, nt), QT[nx].v(0, nt, 0, nt, r=True), X[cur].v(0, nt, 0, nt, r=True))
                yield
                tt("dve", X[nx].v(0, nt, 0, nt, r=True), X[cur].v(0, nt, 0, nt), PM[2].v(0, nt, 0, nt), ALU.add)
                yield
                cur = nx
            Xf = X[cur]
            tr(vb(PM[3], 0, 128, 0, nt), kn.v(t0, t0 + nt), identr.v(0, 128))
            tr(vb(PM[0], 0, 128, 0, nt), vn.v(t0, t0 + nt), identr.v(0, 128))
            yield
            act(kdec.v(0, 128, 0, nt, r=True), vb(PM[3], 0, 128, 0, nt), AF.Copy, scale=col(40))
            cp("dve", vtm.v(0, 128, 0, nt), vb(PM[0], 0, 128, 0, nt))
            memset("dve", pso.v(0, nt), 0.0)
            yield
            chunks = [(r0, min(nt, r0 + GCH)) for r0 in range(0, nt, GCH)]
            for ci_, (r0, r1) in enumerate(chunks):
                tp = (r0, 0) if nt > GCH else None
                mm(PM[1].v(0, 128, 0, nt), kn.v(t0, t0 + nt, r=True), Sb.v(0, 128))
                mm(PM[3].v(0, 128, 0, nt), qn.v(t0, t0 + nt, r=True), Sb.v(0, 128))
                yield
                stt(Wt.v(0, 128, r0, r1, r=True), PM[1].v(0, 128, r0, r1), col(32, r0, r1), vtm.v(0, 128, r0, r1), ALU.mult, ALU.add)
                act(P1s.v(0, 128, r0, r1, r=True), PM[3].v(0, 128, r0, r1), AF.Copy, scale=col(24, r0, r1))
                yield
                mm(PM[2].v(0, 128, 0, nt), Xf.v(0, nt, r0, r1, r=True), Wt.v(0, 128, r0, r1, r=True), tp=tp)
                mm(pso.v(0, nt), P1s.v(0, 128, r0, r1, r=True), identr.v(0, nt, r0, r1, r=True), start=False, stop=False, inc=True, tp=tp)
                yield
                act(VNt.v(0, 128, r0, r1, r=True), PM[2].v(0, 128, r0, r1), AF.Copy, scale=col(8, r0, r1))
                yield
                mm(pso.v(0, nt), VNt.v(0, 128, r0, r1, r=True), AT.v(0, nt, r0, r1, r=True), start=False, stop=False, inc=True, tp=tp)
                mm(PM[0].v(0, 128), kdec.v(0, 128, r0, r1, r=True), VNt.v(0, 128, r0, r1, r=True), tp=tp)
                yield
                stt(S.v(0, 128), S.v(0, 128), s.v(80 + 8 * ci_ + h, 80 + 8 * ci_ + h + 1), PM[0].v(0, 128), ALU.mult, ALU.add)
                cp("dve", Sb.v(0, 128), S.v(0, 128))
                yield
            cp("act", vn.v(t0, t0 + nt, r=True), pso.v(0, nt))
            yield
        rms_gate_out(vn, ntok, params.v(PR_GNW, PR_GNW + 1), st["g"].v(0, ntok), oaT[h].v(0, ntok))

    def hgrn_pre(h, st, ntok, tls, ps_):
        slot = next_sec(("hgrn", h), lookahead=0)
        held.add(id(slot))
        qh, f_, logf, kp, t1, t2 = tmpFs[ps_]
        enb, bc, eb = f_, t2, t1
        a = proj_fm(slot, 0, ntok, xT)
        sigmoid_chain(t2.v(0, ntok), a.v(0, ntok), t1, t2, ntok)
        stt(qh.v(0, ntok), a.v(0, ntok), float(128.0 ** -0.5), t2.v(0, ntok), ALU.mult, ALU.mult)
        yield
        a = proj_fm(slot, 1, ntok, xT)
        sigmoid_chain(t2.v(0, ntok), a.v(0, ntok), t1, t2, ntok)
        ts("dve", f_.v(0, ntok), t2.v(0, ntok), dpar.v(8 + h, 9 + h), dpar.v(h, h + 1), ALU.mult, ALU.add)
        act(logf.v(0, ntok), f_.v(0, ntok), AF.Ln)
        ts("dve", kp.v(0, ntok), f_.v(0, ntok), -1.0, 1.0, ALU.mult, ALU.add)
        yield
        a = proj_fm(slot, 3, ntok, xT)
        sigmoid_chain(st["g"].v(0, ntok), a.v(0, ntok), t1, t2, ntok)
        yield
        scan(bc.v(0, ntok), cv(C_RST, C_RST + ntok), logf.v(0, ntok), 0.0, ALU.mult, ALU.add)
        act(eb.v(0, ntok), bc.v(0, ntok), AF.Exp)
        act(enb.v(0, ntok), bc.v(0, ntok), AF.Exp, scale=-1.0)
        yield
        tt("dve", st["q"].v(0, ntok, r=True), qh.v(0, ntok), eb.v(0, ntok), ALU.mult)
        tt("dve", st["k"].v(0, ntok, r=True), kp.v(0, ntok), enb.v(0, ntok), ALU.mult)
        if ntok >= HBLK:
            nb = ntok // HBLK
            cp("pool", V(st["e"], st["e"].h[:, 0:nb].rearrange("p (b c) -> p b c", c=1)),
               V(eb, eb.h[:, 0:nb * HBLK].rearrange("p (b c) -> p b c", c=HBLK)[:, :, HBLK - 1:HBLK]))
        else:
            cp("pool", st["e"].v(0, 1), eb.v(ntok - 1, ntok))
        yield
        for ti, (t0, nt) in enumerate(tls):
            a = next_acc()
            for kc in range(8):
                mm(a.v(0, 128, 0, nt), xT[kc].v(t0, t0 + nt), slot.v3(512, kc, kc + 1, 256, 384), start=(kc == 0), stop=(kc == 7))
            cp("act", st["v"].v(ti * 128, ti * 128 + 128, 0, nt, r=True), a.v(0, 128, 0, nt))
            if ti == len(tls) - 1:
                held.discard(id(slot))
            yield

    def hgrn_core(h, k, st, ntok, tls):
        qe, ke, ob_ = st["q"], st["k"], st["v"]
        SR, PM, pso = smR[k], ps_m[k], ps_o[k]
        S, Sb = Sh[h], Shb[h]
        aTm, ketm = SR[1], SR[2]
        for ti, (t0, nt) in enumerate(tls):
            vt = lambda a, b, r0=0, r1=128, r=False: st["v"].v(ti * 128 + a, ti * 128 + b, r0, r1, r)
            mm(PM[1].v(0, nt, 0, nt), ke.v(t0, t0 + nt, r=True), qe.v(t0, t0 + nt, r=True))
            tr(vb(PM[2], 0, 128, 0, nt), ke.v(t0, t0 + nt), identr.v(0, 128))
            yield
            tt("dve", aTm.v(0, nt, 0, nt, r=True), PM[1].v(0, nt, 0, nt), cv(C_BDH, C_BDH + nt, 0, nt), ALU.mult)
            memset("dve", pso.v(0, nt), 0.0)
            yield
            cp("act", ketm.v(0, 128, 0, nt, r=True), vb(PM[2], 0, 128, 0, nt))
            yield
            mm(pso.v(0, nt), vt(0, 128, 0, nt, r=True), aTm.v(0, nt, 0, nt, r=True), start=False, stop=False, inc=True)
            blocks = [(b0, min(nt, b0 + HBLK)) for b0 in range(0, nt, HBLK)]
            for bi, (b0, b1) in enumerate(blocks):
                tp = (b0, 0) if nt > HBLK else None
                gb = (t0 + b0) // HBLK
                ebl = st["e"].v(gb, gb + 1)
                mm(pso.v(b0, b1), Sb.v(0, 128), qe.v(t0 + b0, t0 + b1, r=True), start=False, stop=False, inc=True)
                mm(PM[3].v(0, 128), ketm.v(0, 128, b0, b1, r=True), vt(0, 128, b0, b1, r=True), tp=tp)
                yield
                ts("dve", S.v(0, 128), S.v(0, 128), ebl, None, ALU.mult)
                stt(S.v(0, 128), PM[3].v(0, 128), ebl, S.v(0, 128), ALU.mult, ALU.add)
                cp("act", Sb.v(0, 128), S.v(0, 128))
                yield
            cp("act", ob_.v(t0, t0 + nt, r=True), pso.v(0, nt))
            yield
        rms_gate_out(ob_, ntok, params.v(PR_HNW, PR_HNW + 1), st["g"].v(0, ntok), obT[h].v(0, ntok))

    ngroups = (cfg.seq // G) * cfg.nseq + 1
    for _ in range(ngroups):
        plan_group()

    def store_states(conv_dst, gdn_dst, hgrn_dst):
        pg.dma("sp", conv_dst, halo.h[:, :], reads=[halo], semtile=halo, final=True)
        for h in range(NH):
            pg.dma("sp", gdn_dst[h], Sg[h].h[:, :], reads=[Sg[h]], semtile=Sg[h], final=True)
            pg.dma("sp", hgrn_dst[h], Sh[h].h[:, :], reads=[Sh[h]], semtile=Sh[h], final=True)

    def all_seqs():
        if cfg.phase <= 1:
            raise StopBuild()
        for sq_ in range(cfg.nseq):
            memset("pool", halo.v(0, 72), 0.0)
            for h in range(NH):
                memset("pool", Sg[h].v(0, 128), 0.0)
                memset("pool", Sh[h].v(0, 128), 0.0)
                memset("pool", Sgb[h].v(0, 128), 0.0)
                memset("pool", Shb[h].v(0, 128), 0.0)
            for g0 in range(0, cfg.seq, G):
                base = sq_ * cfg.seq + g0
                emit_group(xp[base:base + G, :], yp[base:base + G, :], G)
            store_states(convp[sq_], gdnp[sq_], hgrnp[sq_])
        pg.dma("sp", halo.h[:, :], convbuf, writes=[halo])
        for h in range(NH):
            pg.dma("sp", Sg[h].h[:, :], sgdn_in[h], writes=[Sg[h]])
            cp("pool", Sgb[h].v(0, 128), Sg[h].v(0, 128))
            pg.dma("sp", Sh[h].h[:, :], shgrn_in[h], writes=[Sh[h]])
            cp("pool", Shb[h].v(0, 128), Sh[h].v(0, 128))
        emit_group(xs, ys, NS)
        store_states(convs[0], gdns[0], hgrns[0])

    try:
        all_seqs()
    except StopBuild:
        pg.dma("sp", ys[0:16, :], x_tm[0].h[0:16, :], reads=[x_tm[0]], semtile=x_tm[0], final=True)
        nd = min(G, 512)
        if cfg.phase >= 3:
            src = oaT + obT if cfg.phase < 5 else mgT + mgT
            for i in range(16):
                stg = tmpF[i % 6]
                cp("dve", stg.v(0, nd), src[i].v(0, nd))
                pg.dma("sp", dbg[i, :, 0:nd], stg.h[:, 0:nd], reads=[stg], semtile=stg, final=True)
        if cfg.phase >= 5:
            pg.dma("sp", yp[0:128, :], x_tm[0].h[0:128, :], reads=[x_tm[0]], semtile=x_tm[0], final=True)

    pg.finish()
    pg.emit()
    return nc, pg


N_CORES = 8
_CACHE = {}


def _host_params(conv_w, hgrn_lb_logits, gdn_norm_w, hgrn_norm_w, a_log, dt_bias):
    p = np.zeros((128, NPAR), np.float32)
    cw = np.asarray(conv_w)[0]
    p[:, PR_CW:PR_CW + 96] = cw.reshape(4, 24, 128).transpose(2, 1, 0).reshape(128, 96)
    lg = np.asarray(hgrn_lb_logits)
    p[:, PR_L0:PR_L0 + 8] = lg[0].reshape(8, 128).T
    p[:, PR_L1:PR_L1 + 8] = lg[1].reshape(8, 128).T
    p[:, PR_GNW] = np.asarray(gdn_norm_w)[0]
    p[:, PR_HNW] = np.asarray(hgrn_norm_w)[0]
    p[:, PR_ALOG:PR_ALOG + 8] = np.asarray(a_log)[0][None, :]
    p[:, PR_DTB:PR_DTB + 8] = np.asarray(dt_bias)[0][None, :]
    return p


def run(cfg, x_prompt, x_sample, cache_gdn_conv, state_gdn, state_hgrn, w_in, conv_w, a_log, dt_bias,
        gdn_norm_w, hgrn_lb_logits, hgrn_norm_w, w_br_a, w_br_b, w_out, ln1_g, ln1_b, w_gate_up,
        w_down, ln2_g, ln2_b, n_cores=N_CORES):
    key = (cfg.nseq, cfg.seq, cfg.g, cfg.ns)
    if key not in _CACHE:
        _CACHE[key] = build(cfg)
    nc, _ = _CACHE[key]
    f = lambda a: np.ascontiguousarray(np.asarray(a, dtype=np.float32))
    params = _host_params(conv_w, hgrn_lb_logits, gdn_norm_w, hgrn_norm_w, a_log, dt_bias)
    lnp = np.concatenate([np.broadcast_to(f(v)[0][None, :], (128, D)) for v in (ln1_g, ln1_b, ln2_g, ln2_b)], axis=1)
    lnp = np.ascontiguousarray(lnp, dtype=np.float32)
    consts = make_consts()
    shared = {"w_in": f(w_in)[0], "w_br_a": f(w_br_a)[0], "w_br_b": f(w_br_b)[0], "w_out": f(w_out)[0],
              "w_gate_up": f(w_gate_up)[0], "w_down": f(w_down)[0], "params": params, "lnp": lnp, "consts": consts}
    xp = f(x_prompt)
    xs_ = f(x_sample)
    cb = f(cache_gdn_conv)[0]
    sg = f(state_gdn)[0]
    sh = f(state_hgrn)[0]
    in_maps = []
    for c in range(n_cores):
        m = dict(shared)
        m["xp"] = np.ascontiguousarray(xp[c * cfg.nseq:(c + 1) * cfg.nseq].reshape(cfg.nseq * cfg.seq, D))
        m["xs"] = np.ascontiguousarray(xs_[c])
        m["convbuf"] = np.ascontiguousarray(cb[c].reshape(3, 24, 128).transpose(2, 1, 0).reshape(128, 72))
        m["sgdn"] = np.ascontiguousarray(sg[c])
        m["shgrn"] = np.ascontiguousarray(sh[c])
        in_maps.append(m)
    res = run_bass_kernel_spmd(nc, in_maps, core_ids=list(range(n_cores)))
    R = res.results

    def conv_back(a):
        n = a.shape[0]
        return np.ascontiguousarray(a.reshape(n, 128, 24, 3).transpose(0, 3, 2, 1).reshape(n, 3, 3072))

    y_prompt = np.concatenate([r["yp"].reshape(cfg.nseq, cfg.seq, D) for r in R], axis=0)
    y_sample = np.stack([r["ys"] for r in R], axis=0)
    conv_p = np.concatenate([conv_back(r["convp"]) for r in R], axis=0)[None]
    gdn_p = np.concatenate([r["gdnp"] for r in R], axis=0)[None]
    hgrn_p = np.concatenate([r["hgrnp"] for r in R], axis=0)[None]
    conv_s = np.concatenate([conv_back(r["convs"]) for r in R], axis=0)[None]
    gdn_s = np.concatenate([r["gdns"] for r in R], axis=0)[None]
    hgrn_s = np.concatenate([r["hgrns"] for r in R], axis=0)[None]
    global LAST_DBG
    LAST_DBG = [r.get("dbg") for r in R]
    outs = (y_prompt, y_sample, conv_p, gdn_p, hgrn_p, conv_s, gdn_s, hgrn_s)
    return tuple(np.ascontiguousarray(o, dtype=np.float32) for o in outs)


def kernel(**inputs):
    cfg = Cfg(nseq=4, seq=2048, g=512, ns=16)
    return run(cfg, **inputs)
```

```python
import numpy as np
import concourse.bass as bass
import concourse.mybir as mybir
from concourse.bass_utils import run_bass_kernel_spmd

F32 = mybir.dt.float32
BF16 = mybir.dt.bfloat16
F32R = mybir.dt.float32r
AF = mybir.ActivationFunctionType
ALU = mybir.AluOpType

P = 128
D = 1024
NH = 8
DFF = 2816
NFF = 22
IN_TOTAL = 10256
OFF_QA, OFF_KA, OFF_VA, OFF_GA, OFF_AB = 0, 1024, 2048, 3072, 4096
OFF_QB, OFF_FB, OFF_IB, OFF_GB, OFF_MGA, OFF_MGB = 4112, 5136, 6160, 7184, 8208, 9232
ALPHA = 2.0 ** 0.25
LN_EPS = 1e-5
RMS_EPS = 1e-6
L2_EPS = 1e-6
GCH = 64
HBLK = 64
NEGBIG = -200.0

C_IDENT, C_SLT, C_TRI, C_NM, C_OFFD, C_CTRI, C_CREV, C_BDH, C_ONES, C_NBI = [i * 128 for i in range(10)]
C_CM = 1280
C_RST = 1282
NCONST = C_RST + 512
PR_CW, PR_L0, PR_L1, PR_GNW, PR_HNW, PR_ALOG, PR_DTB = 0, 96, 104, 112, 113, 114, 122
NPAR = 130


def make_consts():
    c = np.zeros((128, NCONST), np.float32)
    j = np.arange(128)[:, None]
    t = np.arange(128)[None, :]
    same_g = (j // GCH) == (t // GCH)
    same_h = (j // HBLK) == (t // HBLK)
    c[:, C_IDENT:C_IDENT + 128] = (j == t)
    c[:, C_SLT:C_SLT + 128] = (j > t)
    c[:, C_TRI:C_TRI + 128] = (j <= t)
    c[:, C_NM:C_NM + 128] = ~((t >= j) & same_g)
    c[:, C_OFFD:C_OFFD + 128] = (j != t)
    c[:, C_CTRI:C_CTRI + 128] = (j <= t) & same_g
    c[:, C_CREV:C_CREV + 128] = (j > t) & same_g
    c[:, C_BDH:C_BDH + 128] = (t >= j) & same_h
    c[:, C_ONES:C_ONES + 128] = 1.0
    c[:, C_NBI:C_NBI + 128] = NEGBIG * (j == t)
    c[:, C_CM] = (np.arange(128) < GCH)
    c[:, C_CM + 1] = (np.arange(128) >= GCH)
    c[:, C_RST:C_RST + 512] = ((np.arange(512) % HBLK) != 0)[None, :]
    return c


class Ev:
    __slots__ = ("sem", "val", "eng")

    def __init__(self, sem, val, eng):
        self.sem, self.val, self.eng = sem, val, eng


class T:
    def __init__(self, h, name, c0=0, hr=None):
        self.h, self.hr, self.name, self.c0 = h, hr, name, c0
        self.rclass = hr is not None
        self.w = None
        self.r = {}
        self.sem = None
        self.semcnt = 0
        self.const = False
        self.pend = 0
        self.bank = None

    def v(self, a=None, b=None, r0=0, r1=128, r=False):
        h = self.hr if r else self.h
        return V(self, h[r0:r1, self.c0 + a:self.c0 + b], r)

    def v3(self, inner, k0, k1, a, b, r0=0, r1=128, r=False):
        h = self.hr if r else self.h
        ap = h[r0:r1, :].rearrange("p (k c) -> p k c", c=inner)
        return V(self, ap[:, k0:k1, a:b], r)


class V:
    __slots__ = ("t", "ap", "r")

    def __init__(self, t, ap, r=False):
        self.t, self.ap, self.r = t, ap, r


def chk(eng, out):
    return


class Prog:
    ENG = ("pe", "act", "dve", "pool", "sp")

    def __init__(self, nc):
        self.nc = nc
        self.eobj = {"pe": nc.tensor, "act": nc.scalar, "dve": nc.vector, "pool": nc.gpsimd, "sp": nc.sync}
        self.stream = {e: [] for e in self.ENG}
        self.cnt = {e: 0 for e in self.ENG}
        self.esem = {e: nc.alloc_semaphore(f"prog_{e}") for e in self.ENG}
        self.known = {e: {} for e in self.ENG}
        self.pend = {e: [] for e in self.ENG}
        self.nsem = 5
        self.ninst = {e: 0 for e in self.ENG}
        self.final = []
        self.sym = {e: [] for e in self.ENG}

    def _waits(self, eng, reads, writes):
        need = {}

        def add(ev, raw):
            if ev is None:
                return
            if ev.eng == eng and not raw:
                return
            if self.known[eng].get(id(ev.sem), 0) >= ev.val:
                return
            k = id(ev.sem)
            if k not in need or need[k].val < ev.val:
                need[k] = ev

        for t in reads:
            assert t.pend == 0, f"read of tile {t.name} with unsignalled writer"
            add(t.w, True)
        for t in writes:
            add(t.w, False)
            for ev in t.r.values():
                add(ev, False)
        eo = self.eobj[eng]
        for ev in need.values():
            self.known[eng][id(ev.sem)] = ev.val
            self.stream[eng].append((lambda eo=eo, s=ev.sem, v=ev.val: eo.wait_ge(s, v)))
            self.sym[eng].append(("wait", id(ev.sem), ev.val))
            self.ninst[eng] += 1

    def _record(self, ev, reads, writes):
        for t in reads:
            if not t.const:
                k = id(ev.sem)
                if k not in t.r or t.r[k].val < ev.val:
                    t.r[k] = ev
        for t in writes:
            t.w = ev
            t.r = {}

    def op(self, eng, fn, reads=(), writes=(), inc=True):
        reads = [t for t in reads if t is not None]
        writes = [t for t in writes if t is not None]
        for t in list(reads) + list(writes):
            if t.bank is not None and t.bank not in writes:
                writes.append(t.bank)
        self._waits(eng, reads, writes)
        self.ninst[eng] += 1
        if inc:
            self.cnt[eng] += 1
            ev = Ev(self.esem[eng], self.cnt[eng], eng)
            sem = self.esem[eng]
            self.stream[eng].append(lambda: fn().then_inc(sem, 1))
            self.sym[eng].append(("inc", id(sem), 1))
            for (pr, pw) in self.pend[eng]:
                self._record(ev, pr, pw)
                for t in pw:
                    t.pend -= 1
            self.pend[eng] = []
            self._record(ev, reads, writes)
        else:
            self.stream[eng].append(fn)
            for t in writes:
                t.pend += 1
            self.pend[eng].append((reads, writes))

    def dma(self, q, out_ap, in_ap, reads=(), writes=(), semtile=None, final=False, **kw):
        nc = self.nc
        self._waits(q, list(reads), list(writes))
        st = semtile if semtile is not None else (writes[0] if writes else reads[0])
        if st.sem is None:
            st.sem = nc.alloc_semaphore(f"dsem{self.nsem}")
            self.nsem += 1
        st.semcnt += 16
        ev = Ev(st.sem, st.semcnt, "dma")
        eo = self.eobj[q]
        sem = st.sem
        self.stream[q].append(lambda: eo.dma_start(out=out_ap, in_=in_ap, **kw).then_inc(sem, 16))
        self.sym[q].append(("inc", id(sem), 16))
        self.ninst[q] += 1
        self._record(ev, list(reads), list(writes))
        if final:
            self.final.append(ev)

    def finish(self):
        eo = self.eobj["sp"]
        last = {}
        for ev in self.final:
            k = id(ev.sem)
            if k not in last or last[k].val < ev.val:
                last[k] = ev
        for ev in last.values():
            self.stream["sp"].append((lambda s=ev.sem, v=ev.val: eo.wait_ge(s, v)))
            self.sym["sp"].append(("wait", id(ev.sem), ev.val))

    def check(self):
        val = {}
        pc = {e: 0 for e in self.ENG}
        prog = True
        while prog:
            prog = False
            for e in self.ENG:
                st = self.sym[e]
                while pc[e] < len(st):
                    k, sid, v = st[pc[e]]
                    if k == "wait":
                        if val.get(sid, 0) < v:
                            break
                    else:
                        val[sid] = val.get(sid, 0) + v
                    pc[e] += 1
                    prog = True
        stuck = {e: (pc[e], len(self.sym[e]), self.sym[e][pc[e]], val.get(self.sym[e][pc[e]][1], 0))
                 for e in self.ENG if pc[e] < len(self.sym[e])}
        return stuck

    def emit(self):
        nc = self.nc
        with nc.Block() as block:
            @block.tensor
            def _(e):
                for f in self.stream["pe"]:
                    f()

            @block.scalar
            def _(e):
                for f in self.stream["act"]:
                    f()

            @block.vector
            def _(e):
                for f in self.stream["dve"]:
                    f()

            @block.gpsimd
            def _(e):
                for f in self.stream["pool"]:
                    f()

            @block.sync
            def _(e):
                for f in self.stream["sp"]:
                    f()


class Cfg:
    def __init__(self, nseq=4, seq=2048, g=512, ns=16, phase=99):
        self.nseq, self.seq, self.g, self.ns, self.phase = nseq, seq, g, ns, phase


class StopBuild(Exception):
    pass


def section_list():
    secs = []
    for h in range(NH):
        secs.append((("gdn", h), [("w_in", 0, 8, o + 128 * h) for o in (OFF_QA, OFF_KA, OFF_VA, OFF_GA)]))
    for h in range(NH):
        secs.append((("hgrn", h), [("w_in", 0, 8, o + 128 * h) for o in (OFF_QB, OFF_FB, OFF_IB, OFF_GB)]))
    for c in range(8):
        secs.append((("mrg", c), [("w_in", 0, 8, OFF_MGA + 128 * c), ("w_in", 0, 8, OFF_MGB + 128 * c),
                                  ("w_br_a", 0, 8, 128 * c), ("w_br_b", 0, 8, 128 * c)]))
    for ch in range(2):
        secs.append((("wout", ch), [("w_out", 0, 8, 512 * ch + 128 * p) for p in range(4)]))
    for s in range(NFF // 2):
        secs.append((("up", s), [("w_gate_up", 0, 8, 128 * (2 * s)), ("w_gate_up", 0, 8, DFF + 128 * (2 * s)),
                                 ("w_gate_up", 0, 8, 128 * (2 * s + 1)), ("w_gate_up", 0, 8, DFF + 128 * (2 * s + 1))]))
    for ch in range(2):
        for r in range(3):
            nk = 8 if r < 2 else NFF - 16
            secs.append((("down", ch, r), [("w_down", 1024 * r, nk, 512 * ch + 128 * p) for p in range(4)]))
    return secs


def build(cfg):
    nc = bass.Bass("TRN2", target_bir_lowering=False)
    G = cfg.g
    NT = cfg.nseq * cfg.seq
    NS = cfg.ns

    def din(name, shape):
        return nc.dram_tensor(name, list(shape), F32, kind="ExternalInput").ap()

    def dout(name, shape):
        return nc.dram_tensor(name, list(shape), F32, kind="ExternalOutput").ap()

    xp = din("xp", (NT, D))
    xs = din("xs", (NS, D))
    convbuf = din("convbuf", (P, 72))
    sgdn_in = din("sgdn", (NH, P, P))
    shgrn_in = din("shgrn", (NH, P, P))
    W = {
        "w_in": din("w_in", (D, IN_TOTAL)),
        "w_br_a": din("w_br_a", (D, D)),
        "w_br_b": din("w_br_b", (D, D)),
        "w_out": din("w_out", (D, D)),
        "w_gate_up": din("w_gate_up", (D, 2 * DFF)),
        "w_down": din("w_down", (DFF, D)),
    }
    params_d = din("params", (P, NPAR))
    lnp_d = din("lnp", (P, 4 * D))
    consts_d = din("consts", (P, NCONST))

    yp = dout("yp", (NT, D))
    ys = dout("ys", (NS, D))
    convp = dout("convp", (cfg.nseq, P, 72))
    gdnp = dout("gdnp", (cfg.nseq, NH, P, P))
    hgrnp = dout("hgrnp", (cfg.nseq, NH, P, P))
    convs = dout("convs", (1, P, 72))
    gdns = dout("gdns", (1, NH, P, P))
    hgrns = dout("hgrns", (1, NH, P, P))

    dbg = dout("dbg", (16, P, 512)) if cfg.phase < 99 else None
    secs = section_list()
    NSEC = len(secs)
    wscr = nc.dram_tensor("wscr", [NSEC, P, 4096], BF16).ap()

    pg = Prog(nc)
    _n = [0]

    def sb(shape_w, dt=F32, name=None, r=False):
        _n[0] += 1
        nm = f"{name or 'sb'}{_n[0]}"
        if r:
            dt = BF16
        h = nc.alloc_sbuf_tensor(nm, [P, shape_w], dt)
        t_ = T(h, nm, 0, h if r else None)
        t_.rclass = False
        return t_

    banks = [nc.alloc_psum_tensor(f"bank{i}", [P, 512], F32) for i in range(8)]
    block_ = [T(None, f"banklock{i}") for i in range(8)]

    banks_b = [b_.bitcast(BF16) for b_ in banks]

    def pst(bi, name, c0=0):
        t = T(banks[bi], name, c0)
        t.bank = block_[bi]
        t.hb = banks_b[bi]
        return t

    def vb(t, a, b, r0=0, r1=128):
        return V(t, t.hb[r0:r1, 2 * t.c0 + a:2 * t.c0 + b])

    acc = [pst(i, f"acc{i}") for i in range(3)]
    acc_pools = {"dense": [0, 1, 2], "pre": [0, 1], "rms": [2]}
    acc_i = {"dense": 0, "pre": 0, "rms": 0}
    acc_mode = ["dense"]

    def next_acc(pool=None):
        pool = pool or acc_mode[0]
        lst = acc_pools[pool]
        t = acc[lst[acc_i[pool] % len(lst)]]
        acc_i[pool] += 1
        return t

    NCORE = 4
    ps_o = [pst(3, f"pso{k}", 128 * k) for k in range(NCORE)]
    ps_m = [[pst(4 + k, f"psm{k}_{q}", 128 * q) for q in range(4)] for k in range(NCORE)]

    consts = sb(NCONST, name="consts")
    params = sb(NPAR, name="params")
    lnp = sb(2 * D, name="lnp")
    identr = sb(128, name="identr", r=True)
    onesr = sb(128, name="onesr", r=True)
    pg.dma("sp", consts.h[:, :], consts_d, writes=[consts])
    pg.dma("sp", params.h[:, :], params_d, writes=[params])

    def rd(*vs):
        return [v.t for v in vs if isinstance(v, V)]

    def apx(x):
        return x.ap if isinstance(x, V) else x

    def mm(out, lhsT, rhs, start=True, stop=True, inc=None, tp=None):
        kw = {}
        if tp is not None:
            kw["tile_position"] = tp
        pg.op("pe", lambda: nc.tensor.matmul(out.ap, lhsT.ap, rhs.ap, start=start, stop=stop, **kw),
              reads=rd(lhsT, rhs), writes=[out.t], inc=(stop if inc is None else inc))

    def tr(out, in_, ident):
        pg.op("pe", lambda: nc.tensor.transpose(out.ap, in_.ap, ident.ap), reads=rd(in_, ident), writes=[out.t])

    def act(out, in_, func, bias=0.0, scale=1.0, accum=None):
        kw = {}
        if accum is not None:
            kw["accum_out"] = accum.ap
        wr = [out.t] + ([accum.t] if accum is not None else [])
        chk("act", out)
        pg.op("act", lambda: nc.scalar.activation(out=out.ap, in_=in_.ap, func=func, bias=apx(bias), scale=apx(scale), **kw),
              reads=rd(in_, bias, scale), writes=wr)

    def ts(eng, out, in0, s1, s2, op0, op1=None):
        eo = pg.eobj[eng]
        chk(eng, out)
        if op1 is None:
            pg.op(eng, lambda: eo.tensor_scalar(out=out.ap, in0=in0.ap, scalar1=apx(s1), scalar2=None, op0=op0),
                  reads=rd(in0, s1), writes=[out.t])
        else:
            pg.op(eng, lambda: eo.tensor_scalar(out=out.ap, in0=in0.ap, scalar1=apx(s1), scalar2=apx(s2), op0=op0, op1=op1),
                  reads=rd(in0, s1, s2), writes=[out.t])

    def tt(eng, out, a, b, op):
        eo = pg.eobj[eng]
        chk(eng, out)
        pg.op(eng, lambda: eo.tensor_tensor(out=out.ap, in0=a.ap, in1=b.ap, op=op), reads=rd(a, b), writes=[out.t])

    def stt(out, in0, scalar, in1, op0, op1):
        chk("dve", out)
        pg.op("dve", lambda: nc.vector.scalar_tensor_tensor(out=out.ap, in0=in0.ap, scalar=apx(scalar), in1=in1.ap, op0=op0, op1=op1),
              reads=rd(in0, scalar, in1), writes=[out.t])

    def scan(out, d0, d1, init, op0, op1):
        pg.op("dve", lambda: nc.vector.tensor_tensor_scan(out=out.ap, data0=d0.ap, data1=d1.ap, initial=init, op0=op0, op1=op1),
              reads=rd(d0, d1), writes=[out.t])

    def recip(out, in_):
        pg.op("dve", lambda: nc.vector.reciprocal(out=out.ap, in_=in_.ap), reads=rd(in_), writes=[out.t])

    def cp(eng, out, in_):
        if eng == "act":
            act(out, in_, AF.Copy)
        else:
            eo = pg.eobj[eng]
            chk(eng, out)
            pg.op(eng, lambda: eo.tensor_copy(out=out.ap, in_=in_.ap), reads=rd(in_), writes=[out.t])

    def memset(eng, out, val):
        eo = pg.eobj[eng]
        chk(eng, out)
        pg.op(eng, lambda: eo.memset(out.ap, val), writes=[out.t])

    def sigmoid_chain(out, in_, tmp1, tmp2, n, rows=128):
        act(tmp1.v(0, n, 0, rows), in_, AF.Exp, scale=-1.0)
        act(tmp2.v(0, n, 0, rows), tmp1.v(0, n, 0, rows), AF.Ln, bias=1.0)
        act(out, tmp2.v(0, n, 0, rows), AF.Exp, scale=-1.0)

    cv = lambda a, b, r0=0, r1=128: consts.v(a, b, r0, r1)

    cp("dve", identr.v(0, 128, r=True), cv(C_IDENT, C_IDENT + 128))
    cp("dve", onesr.v(0, 128, r=True), cv(C_ONES, C_ONES + 128))

    dpar = sb(32, name="dpar")
    tt("dve", dpar.v(24, 32), params.v(PR_L1, PR_L1 + 8), params.v(PR_L0, PR_L0 + 8), ALU.subtract)
    act(dpar.v(24, 32), dpar.v(24, 32), AF.Exp)
    ts("dve", dpar.v(24, 32), dpar.v(24, 32), 1.0, None, ALU.add)
    recip(dpar.v(0, 8), dpar.v(24, 32))
    ts("dve", dpar.v(8, 16), dpar.v(0, 8), -1.0, 1.0, ALU.mult, ALU.add)
    act(dpar.v(16, 24), params.v(PR_ALOG, PR_ALOG + 8), AF.Exp)
    ts("dve", dpar.v(16, 24), dpar.v(16, 24), -1.0, None, ALU.mult)
    for t_ in (consts, params, identr, onesr):
        t_.const = True

    x_tm_sets = [[sb(D, name="xtm") for _ in range(max(1, G // 128))] for _ in range(2)]
    x_tm = x_tm_sets[0]
    NXT = len(x_tm)
    xsel = [0]
    preload = [False]
    xT = [sb(G, BF16, name="xT") for _ in range(8)]
    NRING = 3
    wring = [sb(4096, BF16, name="wring") for _ in range(NRING)]
    wab = sb(128, BF16, name="wab")
    tmpFs = [[sb(G + 4, name="tmpF") for _ in range(6)] for _ in range(2)]
    tmpRs = [sb(G + 4, name="tmpR", r=True) for _ in range(2)]
    ubs = [[sb(G + 4, name="ub", r=True) for _ in range(3)] for _ in range(2)]
    dgs = [[sb(128, name="dg", r=True) for _ in range(4)] for _ in range(2)]
    tmpF = tmpFs[0]
    tmpR = tmpRs[0]
    store = [[dict(q=sb(G + 4, name="stq", r=True), k=sb(G + 4, name="stk", r=True),
                   v=sb(G + 4, name="stv", r=True), g=sb(G + 4, BF16, name="stg"), e=sb(16, name="ebl")) for k_ in range(NCORE)]
             for _ in range(2)]
    rmsR = sb(G + 4, name="rmsR", r=True)
    rms1 = sb(G + 4, name="rms1")
    smF = [[sb(128, name="smF") for _ in range(4)] for _ in range(NCORE)]
    smR = [[sb(128, name="smR", r=True) for _ in range(7)] for _ in range(NCORE)]
    tsc = [sb(96, name="tsc") for _ in range(NXT)]
    Sg = [sb(128, name="Sg") for _ in range(NH)]
    Sh = [sb(128, name="Sh") for _ in range(NH)]
    Sgb = [sb(128, name="Sgb", r=True) for _ in range(NH)]
    Shb = [sb(128, name="Shb", r=True) for _ in range(NH)]
    halo = sb(72, name="halo")
    sstage = sb(128, name="sstage")
    oaT = [sb(G, BF16, name="oaT") for _ in range(8)]
    obT = [sb(G, BF16, name="obT") for _ in range(8)]
    mgT = [sb(G, BF16, name="mgT") for _ in range(8)]
    actT = oaT + obT + mgT[:6]
    lnsc = [sb(16, name="lnsc") for _ in range(NXT)]
    lnjunk = sb(D, BF16, name="lnjunk")
    dtile = {n: T(None, "dram_" + str(n)) for n in range(NSEC)}

    cast_engs = ["dve", "pool", "act"]
    ci = 0
    stg_i = 0
    st = x_tm[stg_i % NXT]
    stg_i += 1
    pg.dma("sp", st.h[:, 0:128].rearrange("p (k c) -> p k c", c=16),
           W["w_in"][:, OFF_AB:OFF_AB + 16].rearrange("(k p) c -> p k c", p=128), writes=[st])
    cp("dve", wab.v(0, 128), st.v(0, 128))
    wab.const = True
    for si, (nm, pieces) in enumerate(secs):
        slot = wring[si % NRING]
        for pi, (wn, r0, nk, c0) in enumerate(pieces):
            st = x_tm[stg_i % NXT]
            stg_i += 1
            pg.dma("sp", st.h[:, 0:nk * 128].rearrange("p (k c) -> p k c", c=128),
                   W[wn][r0:r0 + nk * 128, c0:c0 + 128].rearrange("(k p) c -> p k c", p=128), writes=[st])
            cp(cast_engs[ci % 3], slot.v3(512, 0, nk, pi * 128, pi * 128 + 128), st.v3(128, 0, nk, 0, 128))
            ci += 1
        nk = pieces[0][2]
        pg.dma("pool", wscr[si, :, 0:nk * 512], slot.h[:, 0:nk * 512], reads=[slot], writes=[dtile[si]], semtile=dtile[si])

    sec_idx = {nm: i for i, (nm, _) in enumerate(secs)}
    use_seq = []
    loaded = [0]
    slot_of = {}

    def plan_group():
        for (nm, _) in secs:
            use_seq.append(sec_idx[nm])

    def ensure(k):
        while loaded[0] <= k and loaded[0] < len(use_seq):
            u = loaded[0]
            si = use_seq[u]
            slot = wring[u % NRING]
            assert id(slot) not in held, "weight ring slot still held by a pre stream"
            nk = secs[si][1][0][2]
            pg.dma("sp", slot.h[:, 0:nk * 512], wscr[si, :, 0:nk * 512], reads=[dtile[si]], writes=[slot])
            slot_of[u] = slot
            loaded[0] += 1

    use_ptr = [0]
    held = set()

    def next_sec(expect, lookahead=NRING - 1):
        u = use_ptr[0]
        assert secs[use_seq[u]][0] == expect, (secs[use_seq[u]][0], expect)
        ensure(min(u + lookahead, len(use_seq) - 1))
        use_ptr[0] += 1
        return slot_of[u]

    def next_secs(expects):
        u = use_ptr[0]
        ensure(min(u + NRING - 1, len(use_seq) - 1))
        out = []
        for e in expects:
            assert secs[use_seq[use_ptr[0]]][0] == e
            out.append(slot_of[use_ptr[0]])
            use_ptr[0] += 1
        return out

    def tiles_of(ntok):
        return [(t0, min(128, ntok - t0)) for t0 in range(0, ntok, 128)]

    def proj_fm(slot, piece, ntok, src, nk=8):
        a = next_acc()
        for kc in range(nk):
            mm(a.v(0, ntok), slot.v3(512, kc, kc + 1, piece * 128, piece * 128 + 128), src[kc].v(0, ntok),
               start=(kc == 0), stop=(kc == nk - 1))
        return a

    def rms_gate_out(osb, ntok, normw_col, gate_v, dst):
        sq, t1, t2 = rmsR, rms1, rms1
        act(sq.v(0, ntok, r=True), osb.v(0, ntok), AF.Square)
        a = next_acc("rms")
        mm(a.v(0, ntok), onesr.v(0, 128, r=True), sq.v(0, ntok, r=True))
        act(t1.v(0, ntok), a.v(0, ntok), AF.Ln, bias=RMS_EPS, scale=1.0 / 128)
        act(t1.v(0, ntok), t1.v(0, ntok), AF.Exp, scale=-0.5)
        tt("dve", t2.v(0, ntok), osb.v(0, ntok), t1.v(0, ntok), ALU.mult)
        stt(dst, t2.v(0, ntok), normw_col, gate_v, ALU.mult, ALU.mult)

    def run_cores(gens):
        gens = list(gens)
        while gens:
            for g_ in list(gens):
                try:
                    next(g_)
                except StopIteration:
                    gens.remove(g_)

    def emit_group(x_src, y_dst, ntok, nxt_=None):
        tls = tiles_of(ntok)
        x_tm = x_tm_sets[xsel[0]]
        if not preload[0]:
            for ti, (t0, nt) in enumerate(tls):
                pg.dma("sp", x_tm[ti].h[0:nt, :], x_src[t0:t0 + nt, :], writes=[x_tm[ti]])
        preload[0] = False
        for kc in range(8):
            a = next_acc()
            for ti, (t0, nt) in enumerate(tls):
                tr(a.v(t0, t0 + nt), x_tm[ti].v(kc * 128, kc * 128 + 128, 0, nt), cv(C_IDENT, C_IDENT + nt, 0, nt))
            cp("act" if kc % 2 else "dve", xT[kc].v(0, ntok), a.v(0, ntok))
        def stage_b():
            for ti, (t0, nt) in enumerate(tls):
                s = tsc[ti]
                a = ps_m[0][0]
                for kc in range(8):
                    mm(a.v(0, 16, 0, nt), xT[kc].v(t0, t0 + nt), wab.v(kc * 16, kc * 16 + 16), start=(kc == 0), stop=(kc == 7))
                tt("dve", s.v(64, 72, 0, nt), a.v(0, 8, 0, nt), params.v(PR_DTB, PR_DTB + 8, 0, nt), ALU.add)
                act(s.v(64, 72, 0, nt), s.v(64, 72, 0, nt), AF.Exp)
                act(s.v(64, 72, 0, nt), s.v(64, 72, 0, nt), AF.Ln, bias=1.0)
                tt("dve", s.v(0, 8, 0, nt), s.v(64, 72, 0, nt), dpar.v(16, 24, 0, nt), ALU.mult)
                act(s.v(72, 80, 0, nt), a.v(8, 16, 0, nt), AF.Exp, scale=-1.0)
                ts("dve", s.v(72, 80, 0, nt), s.v(72, 80, 0, nt), 1.0, None, ALU.add)
                recip(s.v(8, 16, 0, nt), s.v(72, 80, 0, nt))
                ts("dve", s.v(16, 24, 0, nt), s.v(8, 16, 0, nt), -1.0, None, ALU.mult)
                yield
                b = ps_m[0][1]
                mm(b.v(0, 8, 0, nt), cv(C_CTRI, C_CTRI + nt, 0, nt), s.v(0, 8, 0, nt))
                act(s.v(24, 32, 0, nt), b.v(0, 8, 0, nt), AF.Exp)
                ts("dve", s.v(32, 40, 0, nt), s.v(24, 32, 0, nt), -1.0, None, ALU.mult)
                c = ps_m[0][2]
                mm(c.v(0, 8, 0, nt), cv(C_CREV, C_CREV + nt, 0, nt), s.v(0, 8, 0, nt))
                act(s.v(40, 48, 0, nt), c.v(0, 8, 0, nt), AF.Exp)
                for ch in range(2):
                    ts("dve", s.v(48 + 8 * ch, 56 + 8 * ch, 0, nt), s.v(0, 8, 0, nt), cv(C_CM + ch, C_CM + ch + 1, 0, nt), None, ALU.mult)
                d = ps_m[0][3]
                mm(d.v(0, 16), cv(C_ONES, C_ONES + 128, 0, nt), s.v(48, 64, 0, nt))
                act(s.v(80, 96), d.v(0, 16), AF.Exp)
                yield

        if cfg.phase <= 2.0:
            raise StopBuild()
        tasks = [("g", h) for h in range(NH)] + [("h", h) for h in range(NH)]
        nw = len(tasks) // NCORE

        free_store = [store[p_][k_] for p_ in range(2) for k_ in range(NCORE)]
        free_core = list(range(NCORE))
        ready = []
        pre_act = [None, None]
        core_act = []
        nxt = [0]

        def mk_pre(task, st, ps_):
            kind, h = task
            return gdn_pre(h, st, ntok, ps_) if kind == "g" else hgrn_pre(h, st, ntok, tls, ps_)

        def mk_core(task, k, st):
            kind, h = task
            return gdn_core(h, k, st, ntok, tls) if kind == "g" else hgrn_core(h, k, st, ntok, tls)

        acc_mode[0] = "pre"
        sb_gen = [stage_b()]
        while True:
            if sb_gen[0] is not None:
                try:
                    next(sb_gen[0])
                except StopIteration:
                    sb_gen[0] = None
            for ps_ in range(2):
                if pre_act[ps_] is None and nxt[0] < len(tasks) and free_store:
                    st = free_store.pop(0)
                    task = tasks[nxt[0]]
                    nxt[0] += 1
                    pre_act[ps_] = (mk_pre(task, st, ps_), task, st)
            while ready and free_core and sb_gen[0] is None:
                k = free_core.pop(0)
                task, st = ready.pop(0)
                core_act.append((mk_core(task, k, st), k, st))
            if not core_act and pre_act[0] is None and pre_act[1] is None and sb_gen[0] is None:
                assert nxt[0] == len(tasks) and not ready
                break
            for ps_ in range(2):
                if pre_act[ps_] is not None:
                    g_, task, st = pre_act[ps_]
                    try:
                        next(g_)
                    except StopIteration:
                        ready.append((task, st))
                        pre_act[ps_] = None
            for item in list(core_act):
                g_, k, st = item
                try:
                    next(g_)
                except StopIteration:
                    core_act.remove(item)
                    free_core.append(k)
                    free_store.append(st)
        acc_mode[0] = "dense"
        if nxt_ is not None:
            nx_src, nx_ntok = nxt_
            for ti, (t0, nt) in enumerate(tiles_of(nx_ntok)):
                xt_ = x_tm_sets[1 - xsel[0]][ti]
                pg.dma("sp", xt_.h[0:nt, :], nx_src[t0:t0 + nt, :], writes=[xt_])
            preload[0] = True
        if cfg.phase <= 4:
            raise StopBuild()
        B = tmpF
        for c in range(8):
            slot = next_sec(("mrg", c))
            p1 = proj_fm(slot, 0, ntok, xT)
            sigmoid_chain(B[0].v(0, ntok), p1.v(0, ntok), B[0], B[0], ntok)
            p2 = proj_fm(slot, 1, ntok, xT)
            sigmoid_chain(B[1].v(0, ntok), p2.v(0, ntok), B[1], B[1], ntok)
            p3 = proj_fm(slot, 2, ntok, oaT)
            tt("dve", B[2].v(0, ntok), p3.v(0, ntok), B[0].v(0, ntok), ALU.mult)
            p4 = proj_fm(slot, 3, ntok, obT)
            tt("dve", B[3].v(0, ntok), p4.v(0, ntok), B[1].v(0, ntok), ALU.mult)
            tt("dve", mgT[c].v(0, ntok), B[2].v(0, ntok), B[3].v(0, ntok), ALU.add)
        for ch in range(2):
            slot = next_sec(("wout", ch))
            for ti, (t0, nt) in enumerate(tls):
                a = next_acc()
                for kc in range(8):
                    mm(a.v(0, 512, 0, nt), mgT[kc].v(t0, t0 + nt), slot.v3(512, kc, kc + 1, 0, 512), start=(kc == 0), stop=(kc == 7))
                stt(x_tm[ti].v(ch * 512, ch * 512 + 512, 0, nt), x_tm[ti].v(ch * 512, ch * 512 + 512, 0, nt), ALPHA,
                    a.v(0, 512, 0, nt), ALU.mult, ALU.add)
        pg.dma("sp", lnp.h[:, :], lnp_d[:, 0:2 * D], writes=[lnp])
        for ti, (t0, nt) in enumerate(tls):
            layer_norm(x_tm[ti], nt, lnsc[ti], 0, None)
        if cfg.phase <= 5:
            raise StopBuild()
        for kc in range(8):
            a = next_acc()
            for ti, (t0, nt) in enumerate(tls):
                tr(a.v(t0, t0 + nt), x_tm[ti].v(kc * 128, kc * 128 + 128, 0, nt), cv(C_IDENT, C_IDENT + nt, 0, nt))
            cp("act" if kc % 2 else "dve", xT[kc].v(0, ntok), a.v(0, ntok))
        for s_ in range(NFF // 2):
            slot = next_sec(("up", s_))
            for jj in range(2):
                j = 2 * s_ + jj
                Bq = tmpF[3 * jj:3 * jj + 3]
                pg_ = proj_fm(slot, 2 * jj, ntok, xT)
                pu_ = proj_fm(slot, 2 * jj + 1, ntok, xT)
                sigmoid_chain(Bq[0].v(0, ntok), pg_.v(0, ntok), Bq[0], Bq[0], ntok)
                tt("dve", Bq[1].v(0, ntok), pg_.v(0, ntok), Bq[0].v(0, ntok), ALU.mult)
                tt("dve", actT[j].v(0, ntok), pu_.v(0, ntok), Bq[1].v(0, ntok), ALU.mult)
        for ch in range(2):
            slots = next_secs([("down", ch, r) for r in range(3)])
            for ti, (t0, nt) in enumerate(tls):
                a = next_acc()
                for j in range(NFF):
                    mm(a.v(0, 512, 0, nt), actT[j].v(t0, t0 + nt), slots[j // 8].v3(512, j % 8, j % 8 + 1, 0, 512),
                       start=(j == 0), stop=(j == NFF - 1))
                stt(x_tm[ti].v(ch * 512, ch * 512 + 512, 0, nt), x_tm[ti].v(ch * 512, ch * 512 + 512, 0, nt), ALPHA,
                    a.v(0, 512, 0, nt), ALU.mult, ALU.add)
        pg.dma("sp", lnp.h[:, :], lnp_d[:, 2 * D:4 * D], writes=[lnp])
        for ti, (t0, nt) in enumerate(tls):
            layer_norm(x_tm[ti], nt, lnsc[ti], 2, None)
            pg.dma("sp", y_dst[t0:t0 + nt, :], x_tm[ti].h[0:nt, :], reads=[x_tm[ti]], semtile=x_tm[ti], final=True)
        xsel[0] ^= 1

    def layer_norm(xt, nt, sc, which, B):
        junk = lnjunk
        act(junk.v(0, D, 0, nt), xt.v(0, D, 0, nt), AF.Copy, accum=sc.v(0, 1, 0, nt))
        act(junk.v(0, D, 0, nt), xt.v(0, D, 0, nt), AF.Square, accum=sc.v(1, 2, 0, nt))
        ts("dve", sc.v(8, 9, 0, nt), sc.v(0, 1, 0, nt), 1.0 / D, None, ALU.mult)
        tt("dve", sc.v(9, 10, 0, nt), sc.v(8, 9, 0, nt), sc.v(8, 9, 0, nt), ALU.mult)
        stt(sc.v(10, 11, 0, nt), sc.v(1, 2, 0, nt), 1.0 / D, sc.v(9, 10, 0, nt), ALU.mult, ALU.subtract)
        act(sc.v(11, 12, 0, nt), sc.v(10, 11, 0, nt), AF.Ln, bias=LN_EPS)
        act(sc.v(11, 12, 0, nt), sc.v(11, 12, 0, nt), AF.Exp, scale=-0.5)
        stt(sc.v(12, 13, 0, nt), sc.v(8, 9, 0, nt), -1.0, sc.v(11, 12, 0, nt), ALU.mult, ALU.mult)
        act(xt.v(0, D, 0, nt), xt.v(0, D, 0, nt), AF.Identity, bias=sc.v(12, 13, 0, nt), scale=sc.v(11, 12, 0, nt))
        tt("dve", xt.v(0, D, 0, nt), xt.v(0, D, 0, nt), lnp.v(0, D, 0, nt), ALU.mult)
        tt("dve", xt.v(0, D, 0, nt), xt.v(0, D, 0, nt), lnp.v(D, 2 * D, 0, nt), ALU.add)

    def silu_from(out, y, t1, t2, n):
        sigmoid_chain(t2.v(0, n), y, t1, t2, n)
        tt("dve", out, y, t2.v(0, n), ALU.mult)

    def gdn_pre(h, st, ntok, ps_):
        slot = next_sec(("gdn", h), lookahead=0)
        held.add(id(slot))
        TF, TR, ub, dg = tmpFs[ps_], tmpRs[ps_], ubs[ps_], dgs[ps_]
        t1, t2, cq, ck = TF[0], TF[1], TF[2], TF[3]
        for i in range(3):
            a = proj_fm(slot, i, ntok, xT)
            ch = i * 8 + h
            cp("pool", ub[i].v(0, 3), halo.v(ch * 3, ch * 3 + 3))
            cp("act", ub[i].v(3, 3 + ntok), a.v(0, ntok))
            cp("dve", halo.v(ch * 3, ch * 3 + 3), a.v(ntok - 3, ntok))
            yield
        a = proj_fm(slot, 3, ntok, xT)
        held.discard(id(slot))
        sigmoid_chain(t2.v(0, ntok), a.v(0, ntok), t1, t2, ntok)
        tt("dve", st["g"].v(0, ntok), a.v(0, ntok), t2.v(0, ntok), ALU.mult)
        yield
        cvo = [cq, ck, st["v"]]
        for i in range(3):
            ch = i * 8 + h
            for j in range(4):
                ts("dve", dg[j].v(0, 128), identr.v(0, 128), params.v(PR_CW + ch * 4 + j, PR_CW + ch * 4 + j + 1), None, ALU.mult)
            yield
            a = next_acc()
            for j in range(4):
                mm(a.v(0, ntok), dg[j].v(0, 128), ub[i].v(j, j + ntok), start=(j == 0), stop=(j == 3))
            sigmoid_chain(t2.v(0, ntok), a.v(0, ntok), t1, t2, ntok)
            tt("dve", cvo[i].v(0, ntok), a.v(0, ntok), t2.v(0, ntok), ALU.mult)
            yield
        for src, dst, lb_ in ((cq, st["q"], -0.5 * np.log(128.0)), (ck, st["k"], 0.0)):
            act(TR.v(0, ntok, r=True), src.v(0, ntok), AF.Square)
            a = next_acc()
            mm(a.v(0, ntok), onesr.v(0, 128, r=True), TR.v(0, ntok, r=True))
            act(t1.v(0, ntok), a.v(0, ntok), AF.Ln, bias=L2_EPS)
            act(t1.v(0, ntok), t1.v(0, ntok), AF.Exp, scale=-0.5, bias=float(lb_))
            tt("dve", dst.v(0, ntok, r=True), src.v(0, ntok), t1.v(0, ntok), ALU.mult)
            yield

    def gdn_core(h, k, st, ntok, tls):
        qn, kn, vn = st["q"], st["k"], st["v"]
        SF, SR, PM, pso = smF[k], smR[k], ps_m[k], ps_o[k]
        S, Sb = Sg[h], Sgb[h]
        for ti, (t0, nt) in enumerate(tls):
            s = tsc[ti]
            col = lambda base, r0=0, r1=nt: s.v(base + h, base + h + 1, r0, r1)
            E, Es, lg, vtm = SF[0], SF[1], SF[2], SF[3]
            AT, Q, QT, X = SR[0], [SR[1], SR[2]], [SR[3], SR[4]], [SR[5], SR[6]]
            kdec, Wt, VNt, P1s = SR[1], SR[2], SR[3], SR[4]
            mm(PM[0].v(0, nt, 0, nt), kn.v(t0, t0 + nt, r=True), kn.v(t0, t0 + nt, r=True))
            mm(PM[1].v(0, nt, 0, nt), kn.v(t0, t0 + nt, r=True), qn.v(t0, t0 + nt, r=True))
            ts("dve", lg.v(0, nt, 0, nt), cv(C_SLT, C_SLT + nt, 0, nt), col(0), None, ALU.mult)
            yield
            mm(PM[2].v(0, nt, 0, nt), lg.v(0, nt, 0, nt), cv(C_TRI, C_TRI + nt, 0, nt), start=True, stop=False)
            mm(PM[2].v(0, nt, 0, nt), cv(C_NBI, C_NBI + nt, 0, nt), cv(C_NM, C_NM + nt, 0, nt), start=False, stop=True)
            act(E.v(0, nt, 0, nt), PM[2].v(0, nt, 0, nt), AF.Exp)
            yield
            tt("dve", Es.v(0, nt, 0, nt), E.v(0, nt, 0, nt), cv(C_OFFD, C_OFFD + nt, 0, nt), ALU.mult)
            tt("dve", AT.v(0, nt, 0, nt, r=True), PM[1].v(0, nt, 0, nt), E.v(0, nt, 0, nt), ALU.mult)
            yield
            stt(Q[0].v(0, nt, 0, nt, r=True), PM[0].v(0, nt, 0, nt), col(16), Es.v(0, nt, 0, nt), ALU.mult, ALU.mult)
            yield
            tr(vb(PM[3], 0, nt, 0, nt), Q[0].v(0, nt, 0, nt), identr.v(0, nt, 0, nt))
            tt("dve", X[0].v(0, nt, 0, nt, r=True), Q[0].v(0, nt, 0, nt), cv(C_IDENT, C_IDENT + nt, 0, nt), ALU.add)
            yield
            cp("act", QT[0].v(0, nt, 0, nt, r=True), vb(PM[3], 0, nt, 0, nt))
            yield
            cur = 0
            for lvl in range(5):
                nx = 1 - cur
                if lvl < 4:
                    mm(PM[0].v(0, nt, 0, nt), QT[cur].v(0, nt, 0, nt, r=True), Q[cur].v(0, nt, 0, nt, r=True))
                mm(PM[1].v(0, nt, 0, nt), Q[cur].v(0, nt, 0, nt, r=True), QT[cur].v(0, nt, 0, nt, r=True))
                yield
                if lvl < 4:
                    cp("act", Q[nx].v(0, nt, 0, nt, r=True), PM[0].v(0, nt, 0, nt))
                cp("dve", QT[nx].v(0, nt, 0, nt, r=True), PM[1].v(0, nt, 0, nt))
                yield
                mm(PM[2].v(0, nt, 0, nt), QT[nx].v(0, nt, 0, nt, r=True), X[cur].v(0, nt, 0, nt, r=True))
                yield
                tt("dve", X[nx].v(0, nt, 0, nt, r=True), X[cur].v(0, nt, 0, nt), PM[2].v(0, nt, 0, nt), ALU.add)
                yield
                cur = nx
            Xf = X[cur]
            tr(vb(PM[3], 0, 128, 0, nt), kn.v(t0, t0 + nt), identr.v(0, 128))
            tr(vb(PM[0], 0, 128, 0, nt), vn.v(t0, t0 + nt), identr.v(0, 128))
            yield
            act(kdec.v(0, 128, 0, nt, r=True), vb(PM[3], 0, 128, 0, nt), AF.Copy, scale=col(40))
            cp("dve", vtm.v(0, 128, 0, nt), vb(PM[0], 0, 128, 0, nt))
            memset("dve", pso.v(0, nt), 0.0)
            yield
            chunks = [(r0, min(nt, r0 + GCH)) for r0 in range(0, nt, GCH)]
            for ci_, (r0, r1) in enumerate(chunks):
                tp = (r0, 0) if nt > GCH else None
                mm(PM[1].v(0, 128, 0, nt), kn.v(t0, t0 + nt, r=True), Sb.v(0, 128))
                mm(PM[3].v(0, 128, 0, nt), qn.v(t0, t0 + nt, r=True), Sb.v(0, 128))
                yield
                stt(Wt.v(0, 128, r0, r1, r=True), PM[1].v(0, 128, r0, r1), col(32, r0, r1), vtm.v(0, 128, r0, r1), ALU.mult, ALU.add)
                act(P1s.v(0, 128, r0, r1, r=True), PM[3].v(0, 128, r0, r1), AF.Copy, scale=col(24, r0, r1))
                yield
                mm(PM[2].v(0, 128, 0, nt), Xf.v(0, nt, r0, r1, r=True), Wt.v(0, 128, r0, r1, r=True), tp=tp)
                mm(pso.v(0, nt), P1s.v(0, 128, r0, r1, r=True), identr.v(0, nt, r0, r1, r=True), start=False, stop=False, inc=True, tp=tp)
                yield
                act(VNt.v(0, 128, r0, r1, r=True), PM[2].v(0, 128, r0, r1), AF.Copy, scale=col(8, r0, r1))
                yield
                mm(pso.v(0, nt), VNt.v(0, 128, r0, r1, r=True), AT.v(0, nt, r0, r1, r=True), start=False, stop=False, inc=True, tp=tp)
                mm(PM[0].v(0, 128), kdec.v(0, 128, r0, r1, r=True), VNt.v(0, 128, r0, r1, r=True), tp=tp)
                yield
                stt(S.v(0, 128), S.v(0, 128), s.v(80 + 8 * ci_ + h, 80 + 8 * ci_ + h + 1), PM[0].v(0, 128), ALU.mult, ALU.add)
                cp("dve", Sb.v(0, 128), S.v(0, 128))
                yield
            cp("act", vn.v(t0, t0 + nt, r=True), pso.v(0, nt))
            yield
        rms_gate_out(vn, ntok, params.v(PR_GNW, PR_GNW + 1), st["g"].v(0, ntok), oaT[h].v(0, ntok))

    def hgrn_pre(h, st, ntok, tls, ps_):
        slot = next_sec(("hgrn", h), lookahead=0)
        held.add(id(slot))
        qh, f_, logf, kp, t1, t2 = tmpFs[ps_]
        enb, bc, eb = f_, t2, t1
        a = proj_fm(slot, 0, ntok, xT)
        sigmoid_chain(t2.v(0, ntok), a.v(0, ntok), t1, t2, ntok)
        stt(qh.v(0, ntok), a.v(0, ntok), float(128.0 ** -0.5), t2.v(0, ntok), ALU.mult, ALU.mult)
        yield
        a = proj_fm(slot, 1, ntok, xT)
        sigmoid_chain(t2.v(0, ntok), a.v(0, ntok), t1, t2, ntok)
        ts("dve", f_.v(0, ntok), t2.v(0, ntok), dpar.v(8 + h, 9 + h), dpar.v(h, h + 1), ALU.mult, ALU.add)
        act(logf.v(0, ntok), f_.v(0, ntok), AF.Ln)
        ts("dve", kp.v(0, ntok), f_.v(0, ntok), -1.0, 1.0, ALU.mult, ALU.add)
        yield
        a = proj_fm(slot, 3, ntok, xT)
        sigmoid_chain(st["g"].v(0, ntok), a.v(0, ntok), t1, t2, ntok)
        yield
        scan(bc.v(0, ntok), cv(C_RST, C_RST + ntok), logf.v(0, ntok), 0.0, ALU.mult, ALU.add)
        act(eb.v(0, ntok), bc.v(0, ntok), AF.Exp)
        act(enb.v(0, ntok), bc.v(0, ntok), AF.Exp, scale=-1.0)
        yield
        tt("dve", st["q"].v(0, ntok, r=True), qh.v(0, ntok), eb.v(0, ntok), ALU.mult)
        tt("dve", st["k"].v(0, ntok, r=True), kp.v(0, ntok), enb.v(0, ntok), ALU.mult)
        if ntok >= HBLK:
            nb = ntok // HBLK
            cp("pool", V(st["e"], st["e"].h[:, 0:nb].rearrange("p (b c) -> p b c", c=1)),
               V(eb, eb.h[:, 0:nb * HBLK].rearrange("p (b c) -> p b c", c=HBLK)[:, :, HBLK - 1:HBLK]))
        else:
            cp("pool", st["e"].v(0, 1), eb.v(ntok - 1, ntok))
        yield
        for ti, (t0, nt) in enumerate(tls):
            a = next_acc()
            for kc in range(8):
                mm(a.v(0, 128, 0, nt), xT[kc].v(t0, t0 + nt), slot.v3(512, kc, kc + 1, 256, 384), start=(kc == 0), stop=(kc == 7))
            cp("act", st["v"].v(ti * 128, ti * 128 + 128, 0, nt, r=True), a.v(0, 128, 0, nt))
            if ti == len(tls) - 1:
                held.discard(id(slot))
            yield

    def hgrn_core(h, k, st, ntok, tls):
        qe, ke, ob_ = st["q"], st["k"], st["v"]
        SR, PM, pso = smR[k], ps_m[k], ps_o[k]
        S, Sb = Sh[h], Shb[h]
        aTm, ketm = SR[1], SR[2]
        for ti, (t0, nt) in enumerate(tls):
            vt = lambda a, b, r0=0, r1=128, r=False: st["v"].v(ti * 128 + a, ti * 128 + b, r0, r1, r)
            mm(PM[1].v(0, nt, 0, nt), ke.v(t0, t0 + nt, r=True), qe.v(t0, t0 + nt, r=True))
            tr(vb(PM[2], 0, 128, 0, nt), ke.v(t0, t0 + nt), identr.v(0, 128))
            yield
            tt("dve", aTm.v(0, nt, 0, nt, r=True), PM[1].v(0, nt, 0, nt), cv(C_BDH, C_BDH + nt, 0, nt), ALU.mult)
            memset("dve", pso.v(0, nt), 0.0)
            yield
            cp("act", ketm.v(0, 128, 0, nt, r=True), vb(PM[2], 0, 128, 0, nt))
            yield
            mm(pso.v(0, nt), vt(0, 128, 0, nt, r=True), aTm.v(0, nt, 0, nt, r=True), start=False, stop=False, inc=True)
            blocks = [(b0, min(nt, b0 + HBLK)) for b0 in range(0, nt, HBLK)]
            for bi, (b0, b1) in enumerate(blocks):
                tp = (b0, 0) if nt > HBLK else None
                gb = (t0 + b0) // HBLK
                ebl = st["e"].v(gb, gb + 1)
                mm(pso.v(b0, b1), Sb.v(0, 128), qe.v(t0 + b0, t0 + b1, r=True), start=False, stop=False, inc=True)
                mm(PM[3].v(0, 128), ketm.v(0, 128, b0, b1, r=True), vt(0, 128, b0, b1, r=True), tp=tp)
                yield
                ts("dve", S.v(0, 128), S.v(0, 128), ebl, None, ALU.mult)
                stt(S.v(0, 128), PM[3].v(0, 128), ebl, S.v(0, 128), ALU.mult, ALU.add)
                cp("act", Sb.v(0, 128), S.v(0, 128))
                yield
            cp("act", ob_.v(t0, t0 + nt, r=True), pso.v(0, nt))
            yield
        rms_gate_out(ob_, ntok, params.v(PR_HNW, PR_HNW + 1), st["g"].v(0, ntok), obT[h].v(0, ntok))

    ngroups = (cfg.seq // G) * cfg.nseq + 1
    for _ in range(ngroups):
        plan_group()

    def store_states(conv_dst, gdn_dst, hgrn_dst):
        pg.dma("sp", conv_dst, halo.h[:, :], reads=[halo], semtile=halo, final=True)
        for h in range(NH):
            pg.dma("sp", gdn_dst[h], Sg[h].h[:, :], reads=[Sg[h]], semtile=Sg[h], final=True)
            pg.dma("sp", hgrn_dst[h], Sh[h].h[:, :], reads=[Sh[h]], semtile=Sh[h], final=True)

    def all_seqs():
        if cfg.phase <= 1:
            raise StopBuild()
        for sq_ in range(cfg.nseq):
            memset("pool", halo.v(0, 72), 0.0)
            for h in range(NH):
                memset("pool", Sg[h].v(0, 128), 0.0)
                memset("pool", Sh[h].v(0, 128), 0.0)
                memset("pool", Sgb[h].v(0, 128), 0.0)
                memset("pool", Shb[h].v(0, 128), 0.0)
            for g0 in range(0, cfg.seq, G):
                base = sq_ * cfg.seq + g0
                nb = base + G
                nxt_ = (xp[nb:nb + G, :], G) if nb < NT else (xs, NS)
                emit_group(xp[base:base + G, :], yp[base:base + G, :], G, nxt_)
            store_states(convp[sq_], gdnp[sq_], hgrnp[sq_])
        pg.dma("sp", halo.h[:, :], convbuf, writes=[halo])
        for h in range(NH):
            pg.dma("sp", Sg[h].h[:, :], sgdn_in[h], writes=[Sg[h]])
            cp("pool", Sgb[h].v(0, 128), Sg[h].v(0, 128))
            pg.dma("sp", Sh[h].h[:, :], shgrn_in[h], writes=[Sh[h]])
            cp("pool", Shb[h].v(0, 128), Sh[h].v(0, 128))
        emit_group(xs, ys, NS)
        store_states(convs[0], gdns[0], hgrns[0])

    try:
        all_seqs()
    except StopBuild:
        pg.dma("sp", ys[0:16, :], x_tm[0].h[0:16, :], reads=[x_tm[0]], semtile=x_tm[0], final=True)
        nd = min(G, 512)
        if cfg.phase >= 3:
            src = oaT + obT if cfg.phase < 5 else mgT + mgT
            for i in range(16):
                stg = tmpF[i % 6]
                cp("dve", stg.v(0, nd), src[i].v(0, nd))
                pg.dma("sp", dbg[i, :, 0:nd], stg.h[:, 0:nd], reads=[stg], semtile=stg, final=True)
        if cfg.phase >= 5:
            pg.dma("sp", yp[0:128, :], x_tm[0].h[0:128, :], reads=[x_tm[0]], semtile=x_tm[0], final=True)

    pg.finish()
    pg.emit()
    return nc, pg


N_CORES = 8
_CACHE = {}


def _host_params(conv_w, hgrn_lb_logits, gdn_norm_w, hgrn_norm_w, a_log, dt_bias):
    p = np.zeros((128, NPAR), np.float32)
    cw = np.asarray(conv_w)[0]
    p[:, PR_CW:PR_CW + 96] = cw.reshape(4, 24, 128).transpose(2, 1, 0).reshape(128, 96)
    lg = np.asarray(hgrn_lb_logits)
    p[:, PR_L0:PR_L0 + 8] = lg[0].reshape(8, 128).T
    p[:, PR_L1:PR_L1 + 8] = lg[1].reshape(8, 128).T
    p[:, PR_GNW] = np.asarray(gdn_norm_w)[0]
    p[:, PR_HNW] = np.asarray(hgrn_norm_w)[0]
    p[:, PR_ALOG:PR_ALOG + 8] = np.asarray(a_log)[0][None, :]
    p[:, PR_DTB:PR_DTB + 8] = np.asarray(dt_bias)[0][None, :]
    return p


def run(cfg, x_prompt, x_sample, cache_gdn_conv, state_gdn, state_hgrn, w_in, conv_w, a_log, dt_bias,
        gdn_norm_w, hgrn_lb_logits, hgrn_norm_w, w_br_a, w_br_b, w_out, ln1_g, ln1_b, w_gate_up,
        w_down, ln2_g, ln2_b, n_cores=N_CORES):
    key = (cfg.nseq, cfg.seq, cfg.g, cfg.ns)
    if key not in _CACHE:
        _CACHE[key] = build(cfg)
    nc, _ = _CACHE[key]
    f = lambda a: np.ascontiguousarray(np.asarray(a, dtype=np.float32))
    params = _host_params(conv_w, hgrn_lb_logits, gdn_norm_w, hgrn_norm_w, a_log, dt_bias)
    lnp = np.concatenate([np.broadcast_to(f(v)[0][None, :], (128, D)) for v in (ln1_g, ln1_b, ln2_g, ln2_b)], axis=1)
    lnp = np.ascontiguousarray(lnp, dtype=np.float32)
    consts = make_consts()
    shared = {"w_in": f(w_in)[0], "w_br_a": f(w_br_a)[0], "w_br_b": f(w_br_b)[0], "w_out": f(w_out)[0],
              "w_gate_up": f(w_gate_up)[0], "w_down": f(w_down)[0], "params": params, "lnp": lnp, "consts": consts}
    xp = f(x_prompt)
    xs_ = f(x_sample)
    cb = f(cache_gdn_conv)[0]
    sg = f(state_gdn)[0]
    sh = f(state_hgrn)[0]
    in_maps = []
    for c in range(n_cores):
        m = dict(shared)
        m["xp"] = np.ascontiguousarray(xp[c * cfg.nseq:(c + 1) * cfg.nseq].reshape(cfg.nseq * cfg.seq, D))
        m["xs"] = np.ascontiguousarray(xs_[c])
        m["convbuf"] = np.ascontiguousarray(cb[c].reshape(3, 24, 128).transpose(2, 1, 0).reshape(128, 72))
        m["sgdn"] = np.ascontiguousarray(sg[c])
        m["shgrn"] = np.ascontiguousarray(sh[c])
        in_maps.append(m)
    res = run_bass_kernel_spmd(nc, in_maps, core_ids=list(range(n_cores)))
    R = res.results

    def conv_back(a):
        n = a.shape[0]
        return np.ascontiguousarray(a.reshape(n, 128, 24, 3).transpose(0, 3, 2, 1).reshape(n, 3, 3072))

    y_prompt = np.concatenate([r["yp"].reshape(cfg.nseq, cfg.seq, D) for r in R], axis=0)
    y_sample = np.stack([r["ys"] for r in R], axis=0)
    conv_p = np.concatenate([conv_back(r["convp"]) for r in R], axis=0)[None]
    gdn_p = np.concatenate([r["gdnp"] for r in R], axis=0)[None]
    hgrn_p = np.concatenate([r["hgrnp"] for r in R], axis=0)[None]
    conv_s = np.concatenate([conv_back(r["convs"]) for r in R], axis=0)[None]
    gdn_s = np.concatenate([r["gdns"] for r in R], axis=0)[None]
    hgrn_s = np.concatenate([r["hgrns"] for r in R], axis=0)[None]
    global LAST_DBG
    LAST_DBG = [r.get("dbg") for r in R]
    outs = (y_prompt, y_sample, conv_p, gdn_p, hgrn_p, conv_s, gdn_s, hgrn_s)
    return tuple(np.ascontiguousarray(o, dtype=np.float32) for o in outs)


def kernel(**inputs):
    cfg = Cfg(nseq=4, seq=2048, g=512, ns=16)
    return run(cfg, **inputs)
```
